# Optimizing a Trainium2 kernel written in Bass

```python
import jax, jax.numpy as jnp
from jax import lax
import numpy as np

D_MODEL = 1024
BATCH = 32
SEQ = 2048
DEPTH = 4

CHUNK = 64
Q_BLOCK = 128
EPS = 1e-6
NEG_INF = -1e30

MLA_HEADS = 8
QK_NOPE = 64
QK_ROPE = 32
V_HEAD = 64
Q_LORA = 256
KV_LORA = 128
ROPE_THETA = 10000.0
MLA_WIDTH = MLA_HEADS * V_HEAD

POOL_WINDOWS = (2, 4, 8, 16)
POOL_GROUPS = 4
POOL_GROUP_DIM = 64
POOL_WIDTH = POOL_GROUPS * POOL_GROUP_DIM
POOL_MAX_WIN = 16

SGU_BLOCK = 128
SGU_GROUPS = 4
SGU_GROUP_DIM = 64
SGU_WIDTH = SGU_GROUPS * SGU_GROUP_DIM

CONV_K = 3
CONV_WIDTH = 256

N_BRANCH = 4
D_FF = ((8 * D_MODEL + 3 * 256 - 1) // (3 * 256)) * 256

IN_SIZES = (Q_LORA, KV_LORA, QK_ROPE, POOL_WIDTH, SGU_WIDTH, SGU_WIDTH,
            CONV_WIDTH, CONV_WIDTH, CONV_WIDTH, N_BRANCH * D_MODEL)
IN_WIDTH = sum(IN_SIZES)

kernel_name = "hybrid_gated_mla_pool_sgu_conv_trunk"


def rmsnorm(x, g):
    xf = x.astype(jnp.float32)
    y = xf * lax.rsqrt(jnp.mean(xf * xf, axis=-1, keepdims=True) + EPS)
    return (y * g.astype(jnp.float32)).astype(x.dtype)


def split_cols(p):
    out = []
    o = 0
    for n in IN_SIZES:
        out.append(p[..., o:o + n])
        o += n
    return out


def rope_tables(seq, dtype):
    inv = ROPE_THETA ** (-jnp.arange(0, QK_ROPE, 2, dtype=jnp.float32) / QK_ROPE)
    ang = jnp.arange(seq, dtype=jnp.float32)[:, None] * inv[None, :]
    return jnp.cos(ang).astype(dtype), jnp.sin(ang).astype(dtype)


def apply_rope(x, cos, sin):
    half = x.shape[-1] // 2
    x1, x2 = x[..., :half], x[..., half:]
    return jnp.concatenate([x1 * cos - x2 * sin, x2 * cos + x1 * sin], axis=-1)


def mla_branch(c_q, c_kv, k_r, g_cq, g_ckv, w_uq, w_ukv, cos, sin):
    B, S, _ = c_q.shape
    q = (rmsnorm(c_q, g_cq) @ w_uq).reshape(B, S, MLA_HEADS, QK_NOPE + QK_ROPE)
    q_nope = q[..., :QK_NOPE]
    q_rope = apply_rope(q[..., QK_NOPE:], cos[:, None, :], sin[:, None, :])
    kv = (rmsnorm(c_kv, g_ckv) @ w_ukv).reshape(B, S, MLA_HEADS, QK_NOPE + V_HEAD)
    k_nope, v = kv[..., :QK_NOPE], kv[..., QK_NOPE:]
    k_rope = apply_rope(k_r, cos, sin)
    scale = (QK_NOPE + QK_ROPE) ** -0.5
    chunk_id = jnp.arange(S) // CHUNK
    outs = []
    for blk in range(S // Q_BLOCK):
        q0 = blk * Q_BLOCK
        kend = q0 + Q_BLOCK
        s = (jnp.einsum('bqhd,bkhd->bhqk', q_nope[:, q0:kend], k_nope[:, :kend])
             + jnp.einsum('bqhr,bkr->bhqk', q_rope[:, q0:kend], k_rope[:, :kend]))
        s = s.astype(jnp.float32) * scale
        mask = chunk_id[q0:kend, None] >= chunk_id[None, :kend]
        s = jnp.where(mask[None, None], s, NEG_INF)
        p = jax.nn.softmax(s, axis=-1).astype(v.dtype)
        outs.append(jnp.einsum('bhqk,bkhd->bqhd', p, v[:, :kend]))
    return jnp.concatenate(outs, axis=1).reshape(B, S, MLA_WIDTH)


def pool_branch(z, w_grp, scale):
    B, S, _ = z.shape
    zg = z.reshape(B, S, POOL_GROUPS, POOL_GROUP_DIM).astype(jnp.float32)
    csp = jnp.pad(jnp.cumsum(zg, axis=1), ((0, 0), (POOL_MAX_WIN, 0), (0, 0), (0, 0)))
    t = jnp.arange(S)
    outs = []
    for g, w in enumerate(POOL_WINDOWS):
        win_sum = (csp[:, POOL_MAX_WIN:POOL_MAX_WIN + S, g]
                   - csp[:, POOL_MAX_WIN - w:POOL_MAX_WIN - w + S, g])
        count = jnp.minimum(t + 1, w).astype(jnp.float32)
        outs.append(win_sum / count[None, :, None] - zg[:, :, g])
    pooled = jnp.stack(outs, axis=2).astype(z.dtype)
    mixed = jnp.einsum('bsgc,gcd->bsgd', pooled, w_grp)
    return mixed.reshape(B, S, POOL_WIDTH) * scale


def sgu_branch(u, v, g_v, w_s, b_s):
    B, S, _ = u.shape
    n = S // SGU_BLOCK
    vn = rmsnorm(v, g_v).reshape(B, n, SGU_BLOCK, SGU_GROUPS, SGU_GROUP_DIM)
    pos_chunk = jnp.arange(SGU_BLOCK) // CHUNK
    mask = pos_chunk[:, None] >= pos_chunk[None, :]
    w = jnp.where(mask[None], w_s, 0)
    mixed = jnp.einsum('gij,bnjgc->bnigc', w, vn) + b_s.T[None, None, :, :, None]
    return u * mixed.reshape(B, S, SGU_WIDTH)


def conv_branch(b_gate, c_gate, x_in, w_conv):
    z = c_gate * x_in
    y = lax.conv_general_dilated(z, w_conv, window_strides=(1,),
                                 padding=[(CONV_K - 1, 0)],
                                 dimension_numbers=('NWC', 'WIO', 'NWC'),
                                 feature_group_count=CONV_WIDTH)
    return b_gate * y


def setup_inputs(seed: int = 0) -> dict:
    key = jax.random.key(seed)
    ks = iter(jax.random.split(key, 32))

    def nrm(shape, fan_in):
        return jax.random.normal(next(ks), shape, jnp.float32) * (fan_in ** -0.5)

    def gain(shape):
        return 1.0 + 0.1 * jax.random.normal(next(ks), shape, jnp.float32)

    L = DEPTH
    return {
        "x": jax.random.normal(next(ks), (BATCH, SEQ, D_MODEL), jnp.float32),
        "w_in": nrm((L, D_MODEL, IN_WIDTH), D_MODEL),
        "g_pre_mix": gain((L, D_MODEL)),
        "g_cq": gain((L, Q_LORA)),
        "g_ckv": gain((L, KV_LORA)),
        "w_uq": nrm((L, Q_LORA, MLA_HEADS * (QK_NOPE + QK_ROPE)), Q_LORA),
        "w_ukv": nrm((L, KV_LORA, MLA_HEADS * (QK_NOPE + V_HEAD)), KV_LORA),
        "pool_w": nrm((L, POOL_GROUPS, POOL_GROUP_DIM, POOL_GROUP_DIM), POOL_GROUP_DIM),
        "pool_scale": gain((L, POOL_WIDTH)),
        "g_sgu_v": gain((L, SGU_WIDTH)),
        "sgu_w": nrm((L, SGU_GROUPS, SGU_BLOCK, SGU_BLOCK), SGU_BLOCK),
        "sgu_b": gain((L, SGU_GROUPS, SGU_BLOCK)),
        "conv_w": nrm((L, CONV_K, 1, CONV_WIDTH), CONV_K),
        "w_br_a": nrm((L, MLA_WIDTH, D_MODEL), MLA_WIDTH),
        "w_br_b": nrm((L, POOL_WIDTH, D_MODEL), POOL_WIDTH),
        "w_br_c": nrm((L, SGU_WIDTH, D_MODEL), SGU_WIDTH),
        "w_br_d": nrm((L, CONV_WIDTH, D_MODEL), CONV_WIDTH),
        "w_out": nrm((L, D_MODEL, D_MODEL), D_MODEL),
        "g_post_mix": gain((L, D_MODEL)),
        "g_pre_ffn": gain((L, D_MODEL)),
        "w_ffn_gate": nrm((L, D_MODEL, D_FF), D_MODEL),
        "w_ffn_up": nrm((L, D_MODEL, D_FF), D_MODEL),
        "w_ffn_down": nrm((L, D_FF, D_MODEL), D_FF),
        "g_post_ffn": gain((L, D_MODEL)),
    }


def reference(x, w_in, g_pre_mix, g_cq, g_ckv, w_uq, w_ukv, pool_w, pool_scale,
              g_sgu_v, sgu_w, sgu_b, conv_w, w_br_a, w_br_b, w_br_c, w_br_d,
              w_out, g_post_mix, g_pre_ffn, w_ffn_gate, w_ffn_up, w_ffn_down,
              g_post_ffn):
    B, S, D = x.shape
    cos, sin = rope_tables(S, x.dtype)
    for l in range(DEPTH):
        h = rmsnorm(x, g_pre_mix[l])
        (c_q, c_kv, k_r, p_in, s_u, s_v, cv_b, cv_c, cv_x,
         gate_logits) = split_cols(h @ w_in[l])
        y_a = mla_branch(c_q, c_kv, k_r, g_cq[l], g_ckv[l], w_uq[l], w_ukv[l], cos, sin) @ w_br_a[l]
        y_b = pool_branch(p_in, pool_w[l], pool_scale[l]) @ w_br_b[l]
        y_c = sgu_branch(s_u, s_v, g_sgu_v[l], sgu_w[l], sgu_b[l]) @ w_br_c[l]
        y_d = conv_branch(cv_b, cv_c, cv_x, conv_w[l]) @ w_br_d[l]
        gates = jax.nn.sigmoid(gate_logits.astype(jnp.float32)).astype(x.dtype)
        gates = gates.reshape(B, S, N_BRANCH, D)
        merged = (gates[:, :, 0] * y_a + gates[:, :, 1] * y_b
                  + gates[:, :, 2] * y_c + gates[:, :, 3] * y_d)
        x = x + rmsnorm(merged @ w_out[l], g_post_mix[l])
        h = rmsnorm(x, g_pre_ffn[l])
        f = (jax.nn.silu(h @ w_ffn_gate[l]) * (h @ w_ffn_up[l])) @ w_ffn_down[l]
        x = x + rmsnorm(f, g_post_ffn[l])
    return x
```

```python
import contextlib
import os
import numpy as np
import concourse.bass as bass
import concourse.mybir as mybir
from concourse.bass_utils import run_bass_kernel_spmd

F32 = mybir.dt.float32
BF16 = mybir.dt.bfloat16
AF = mybir.ActivationFunctionType
ALU = mybir.AluOpType
AX = mybir.AxisListType

D = 1024
S = 2048
NCORES = 8
NH = 8
DFF = 2816
NFC = DFF // 128
EPS = 1e-6
SCALE = 96 ** -0.5
H16 = 16

TB = 2
T = TB * 128
NT = S // T

EPOCH = 24000
N_DMA_SEMS = 8
DMA_EPOCH = 1400


class Buf:
    __slots__ = ("name", "w", "r", "rd", "excl")

    def __init__(self, name="", excl=False):
        self.name = name
        self.excl = excl
        self.w = None
        self.r = {}
        self.rd = []


class Op:
    __slots__ = ("eng", "fn", "waits", "signal", "idx", "is_dma", "dma_ev", "tag")

    def __init__(self, eng, fn, is_dma):
        self.eng = eng
        self.fn = fn
        self.waits = []
        self.signal = False
        self.is_dma = is_dma
        self.dma_ev = None


class Prog:
    ENGS = ("pe", "act", "dve", "pool", "sp")

    def __init__(self, nc):
        self.nc = nc
        self.ops = {e: [] for e in self.ENGS}
        self.dma_count = {}
        self.n_dma = {e: 0 for e in self.ENGS}

    def op(self, eng, method, *args, reads=(), writes=(), dma=False, **kwargs):
        self.total = getattr(self, "total", 0) + 1
        if self.total > int(os.environ.get("KMAX", "1000000000")):
            return None
        o = Op(eng, (method, args, kwargs), dma)
        o.tag = getattr(self, "phase", "")
        if os.environ.get("KLOG"):
            print("OP", self.total, eng, method, kwargs.get("func", ""), [b.name for b in writes])
        lst = self.ops[eng]
        o.idx = len(lst)
        deps = []
        for b in reads:
            if b.w is not None:
                deps.append(b.w)
            if b.excl:
                deps.extend(d for e_, d in b.r.items() if e_ != eng)
        for b in writes:
            if b.w is not None:
                deps.append(b.w)
            deps.extend(b.r.values())
            deps.extend(b.rd)
        seen = set()
        for d in deps:
            if id(d) in seen:
                continue
            seen.add(id(d))
            if (not d.is_dma) and d.eng == eng and eng == "pe":
                continue
            o.waits.append(d)
            d.signal = True
        if dma:
            n = self.n_dma[eng]
            self.n_dma[eng] = n + 1
            key = (eng, n % N_DMA_SEMS, (n // N_DMA_SEMS) // DMA_EPOCH)
            c = self.dma_count.get(key, 0) + 1
            self.dma_count[key] = c
            o.dma_ev = (key, 16 * c)
            o.signal = True
        lst.append(o)
        for b in reads:
            if dma:
                b.rd.append(o)
            else:
                b.r[eng] = o
        for b in writes:
            b.w = o
            b.r = {}
            b.rd = []
        return o

    def emit(self, stack):
        nc = self.nc
        sem_of = {}
        semkeys = []
        for e in self.ENGS:
            k = 0
            for o in self.ops[e]:
                if o.is_dma:
                    sem_of[id(o)] = o.dma_ev
                    if o.dma_ev[0] not in semkeys:
                        semkeys.append(o.dma_ev[0])
                elif o.signal:
                    key = (e, "c", k // EPOCH)
                    sem_of[id(o)] = (key, k % EPOCH + 1)
                    if key not in semkeys:
                        semkeys.append(key)
                    k += 1
        sems = {}
        for key in semkeys:
            sems[key] = stack.enter_context(nc.semaphore("s_" + "_".join(str(x) for x in key)))
        final_dma = {key: 16 * c for key, c in self.dma_count.items()}
        block = stack.enter_context(nc.Block())
        prog = self
        self.stats = {}

        prog.emitted = {e: [] for e in prog.ENGS}

        def run(ename, engobj):
            waited = {}
            nw = 0
            em = prog.emitted[ename]
            for o in prog.ops[ename]:
                for d in o.waits:
                    key, val = sem_of[id(d)]
                    if waited.get(key, 0) >= val:
                        continue
                    waited[key] = val
                    engobj.wait_ge(sems[key], val)
                    em.append(("w", o.tag, d.eng, d.tag))
                    nw += 1
                if o.is_dma:
                    key, val = o.dma_ev
                    if val - 16 > 0 and waited.get(key, 0) < val - 16:
                        waited[key] = val - 16
                        engobj.wait_ge(sems[key], val - 16)
                        nw += 1
                m_, a_, k_ = o.fn
                em.append(("i", o.tag, m_, ""))
                inst = getattr(engobj, m_)(*a_, **k_)
                if o.signal:
                    key, val = sem_of[id(o)]
                    inst.then_inc(sems[key], 16 if o.is_dma else 1)
            if ename == "sp":
                for key, val in final_dma.items():
                    if waited.get(key, 0) < val:
                        engobj.wait_ge(sems[key], val)
            prog.stats[ename] = (len(prog.ops[ename]), nw)

        @block.tensor
        def _(eng):
            run("pe", eng)

        @block.scalar
        def _(eng):
            run("act", eng)

        @block.vector
        def _(eng):
            run("dve", eng)

        @block.gpsimd
        def _(eng):
            run("pool", eng)

        @block.sync
        def _(eng):
            run("sp", eng)


def kmaj(w):
    k, m = w.shape
    return w.reshape(k // 128, 128, m).transpose(1, 0, 2).reshape(128, -1)


O_CQ, O_CKV, O_KR, O_P, O_SU, O_SV, O_CB, O_CC, O_CX, O_G = 0, 256, 384, 416, 672, 928, 1184, 1440, 1696, 1952


def mixin_chunks():
    r = np.arange(32)
    ka = np.concatenate([np.arange(64), O_KR + r, np.arange(32)])
    kb = np.concatenate([np.arange(64), O_KR + (r + 16) % 32, np.arange(32)])
    ch = [np.arange(O_CQ, O_CQ + 128), np.arange(O_CQ + 128, O_CQ + 256), np.arange(O_CKV, O_CKV + 128),
          ka, kb,
          np.arange(O_P, O_P + 128), np.arange(O_P + 128, O_P + 256),
          np.arange(O_SU, O_SU + 128), np.arange(O_SU + 128, O_SU + 256),
          np.arange(O_CB, O_CB + 128), np.arange(O_CB + 128, O_CB + 256),
          np.arange(O_CX, O_CX + 128), np.arange(O_CX + 128, O_CX + 256),
          np.arange(O_CC, O_CC + 128), np.arange(O_CC + 128, O_CC + 256)]
    return ch


N_MIX = 15
BLK_MIX0 = 8 * 1024
BLK_MIX1 = 7 * 1024
OFF2_SV, OFF2_WQ, OFF2_WK, OFF2_WV, OFF2_PL = 0, 2048, 2048 + 3072, 2048 + 3072 + 512, 2048 + 3072 + 1024
BLK_2 = 2048 + 3072 + 512 + 512 + 256
BLK_B = 4096 + 1280
BLK_WO = 8192
FFN_GRP = 4
BLK_GU = FFN_GRP * 2048
N_GU = (NFC + FFN_GRP - 1) // FFN_GRP
BLK_DN = 2 * 2816


def block_sizes():
    b = [BLK_MIX0, BLK_MIX1, BLK_2] + [BLK_B] * 8 + [BLK_WO]
    for i in range(N_GU):
        n = min(FFN_GRP, NFC - i * FFN_GRP)
        b.append(n * 2048)
    b += [BLK_DN] * 4
    return b


BLOCKS = block_sizes()
BOFF = np.concatenate([[0], np.cumsum(BLOCKS)]).astype(np.int64)
NCOLS = int(BOFF[-1])
WSLOT = max(BLOCKS)
NV = 48


def pack_layer(l, inp):
    w_in = inp["w_in"][l]
    segs = []
    ch = mixin_chunks()
    for c in ch:
        segs.append(kmaj(w_in[:, c]))
    segs.append(kmaj(w_in[:, O_SV:O_SV + 256]))
    w_uq = inp["w_uq"][l]
    r = np.arange(32)
    cols = []
    for h in range(NH):
        a = h * 96 + np.arange(96)
        b = np.concatenate([h * 96 + np.arange(64), h * 96 + 64 + (r + 16) % 32])
        cols.append(a)
        cols.append(b)
    cols = np.concatenate(cols)
    segs.append(kmaj(w_uq[:, cols]))
    w_ukv = inp["w_ukv"][l]
    kc = np.concatenate([h * 128 + np.arange(64) for h in range(NH)])
    vc = np.concatenate([h * 128 + 64 + np.arange(64) for h in range(NH)])
    segs.append(w_ukv[:, kc])
    segs.append(w_ukv[:, vc])
    pw = inp["pool_w"][l]
    bd = np.zeros((128, 2, 128), np.float32)
    for g in range(4):
        chn, hf = g // 2, g % 2
        bd[hf * 64:(hf + 1) * 64, chn, hf * 64:(hf + 1) * 64] = pw[g]
    segs.append(bd.reshape(128, 256))
    wbr = np.concatenate([inp["w_br_a"][l], inp["w_br_b"][l], inp["w_br_c"][l], inp["w_br_d"][l]], axis=0)
    for d in range(8):
        for b in range(4):
            segs.append(kmaj(w_in[:, O_G + b * 1024 + d * 128:O_G + b * 1024 + (d + 1) * 128]))
        segs.append(kmaj(wbr[:, d * 128:(d + 1) * 128]))
    segs.append(kmaj(inp["w_out"][l]))
    wg, wu, wd = inp["w_ffn_gate"][l], inp["w_ffn_up"][l], inp["w_ffn_down"][l]
    for c in range(NFC):
        segs.append(kmaj(wg[:, c * 128:(c + 1) * 128]))
        segs.append(kmaj(wu[:, c * 128:(c + 1) * 128]))
    for d in range(8):
        segs.append(kmaj(wd[:, d * 128:(d + 1) * 128]))
    out = np.concatenate(segs, axis=1)
    assert out.shape == (128, NCOLS), (out.shape, NCOLS)
    return np.ascontiguousarray(out, dtype=np.float32)


def pack_vecs(l, inp):
    v = np.zeros((128, NV), np.float32)

    def fm(g, n):
        return g.reshape(n, 128).T
    v[:, 0:8] = fm(inp["g_pre_mix"][l], 8)
    v[:, 8:16] = fm(inp["g_post_mix"][l], 8)
    v[:, 16:24] = fm(inp["g_pre_ffn"][l], 8)
    v[:, 24:32] = fm(inp["g_post_ffn"][l], 8)
    v[:, 32:34] = fm(inp["g_cq"][l], 2)
    v[:, 34:35] = fm(inp["g_ckv"][l], 1)
    v[:, 35:37] = fm(inp["pool_scale"][l], 2)
    cw = inp["conv_w"][l][:, 0, :]
    for chn in range(2):
        for k in range(3):
            v[:, 37 + chn * 3 + k] = cw[k, chn * 128:(chn + 1) * 128]
    return v


def pack_consts(L, inp):
    sguw = np.stack([np.ascontiguousarray(inp["sgu_w"][l].transpose(2, 0, 1)).reshape(128, 512) for l in range(L)])
    sb = np.zeros((L, 128, 2, 128), np.float32)
    for l in range(L):
        for g in range(4):
            sb[l, (g % 2) * 64:(g % 2 + 1) * 64, g // 2, :] = inp["sgu_b"][l][g][None, :]
    gsv = np.stack([np.broadcast_to(inp["g_sgu_v"][l][None, :], (128, 256)) for l in range(L)])
    return sguw.astype(np.float32), sb.reshape(L, 128, 256), np.ascontiguousarray(gsv, dtype=np.float32)


def const_tables():
    inv = 10000.0 ** (-np.arange(0, 32, 2, dtype=np.float32) / 32)
    ang = np.arange(S, dtype=np.float32)[:, None] * inv[None, :]
    cos = np.cos(ang).astype(np.float32).T
    sin = np.sin(ang).astype(np.float32).T
    ct = np.zeros((128, S), np.float32)
    st = np.zeros((128, S), np.float32)
    ct[64:80] = cos
    ct[80:96] = cos
    st[64:80] = -sin
    st[80:96] = sin
    pc = np.zeros((128, 2 + 2 * H16), np.float32)
    wins = (2, 4, 8, 16)
    for g in range(4):
        rows = slice((g % 2) * 64, (g % 2 + 1) * 64)
        chn = g // 2
        pc[rows, chn] = 1.0 / wins[g]
        for t in range(H16):
            pc[rows, 2 + chn * H16 + t] = 1.0 / min(t + 1, wins[g])
    return ct, st, pc


def build_nc(nseq, L, ntl=NT):
    nc = bass.Bass("TRN2", target_bir_lowering=False)
    x_in = nc.dram_tensor("x_in", [nseq * S, D], F32, kind="ExternalInput").ap()
    y_out = nc.dram_tensor("y_out", [nseq * S, D], F32, kind="ExternalOutput").ap()
    wf = nc.dram_tensor("wf", [L, 128, NCOLS], F32, kind="ExternalInput").ap()
    vecs = nc.dram_tensor("vecs", [L, 128, NV], F32, kind="ExternalInput").ap()
    sguw_d = nc.dram_tensor("sguw", [L, 128, 512], F32, kind="ExternalInput").ap()
    sgub_d = nc.dram_tensor("sgub", [L, 128, 256], F32, kind="ExternalInput").ap()
    gsv_d = nc.dram_tensor("gsv", [L, 128, 256], F32, kind="ExternalInput").ap()
    ctab = nc.dram_tensor("ctab", [128, S], F32, kind="ExternalInput").ap()
    stab = nc.dram_tensor("stab", [128, S], F32, kind="ExternalInput").ap()
    pcst_d = nc.dram_tensor("pcst", [128, 2 + 2 * H16], F32, kind="ExternalInput").ap()
    wb = nc.dram_tensor("wb", [L, 128, NCOLS], BF16, kind="Internal").ap()
    xs = nc.dram_tensor("xs", [8, 128, S], F32, kind="Internal").ap()

    P = Prog(nc)
    with contextlib.ExitStack() as st:
        def sb(name, shape, dt):
            return st.enter_context(nc.sbuf_tensor("t_" + name, shape, dt))

        KT = [sb(f"KT{h}", [128, S], BF16) for h in range(NH)]
        B_KT = [Buf(f"KT{h}") for h in range(NH)]
        Vc = sb("Vc", [128, S // 128, NH, 65], BF16)
        B_Vc = Buf("Vc")
        identF = sb("identF", [128, 128], F32)
        identB = sb("identB", [128, 128], BF16)
        onesd = sb("onesd", [128, 128], BF16)
        B_const = Buf("const")
        vec = sb("vec", [128, NV], F32)
        sguW = sb("sguW", [128, 4, 128], BF16)
        sguB = sb("sguB", [128, 2, 128], F32)
        gsv = sb("gsv", [128, 256], F32)
        B_lay = Buf("laycst")
        pcst = sb("pcst", [128, 2 + 2 * H16], F32)
        NWS = 3
        wslot = [sb(f"wslot{i}", [128, WSLOT], BF16) for i in range(NWS)]
        B_ws = [Buf(f"ws{i}") for i in range(NWS)]
        xT = [sb(f"xT{i}", [128, 8, T], F32) for i in range(2)]
        B_xT = [[Buf(f"xT{i}_{k}") for k in range(8)] for i in range(2)]
        xtok = sb("xtok", [128, D], F32)
        B_xtok = Buf("xtok")
        hT = sb("hT", [128, 8, T], BF16)
        B_hT = [Buf(f"hT{k}") for k in range(8)]
        sq8 = sb("sq8", [128, 8, T], BF16)
        B_sq8 = [Buf("sq8a"), Buf("sq8b")]
        sq = [sb(f"sq{i}", [128, T], BF16) for i in range(2)]
        B_sq = [Buf(f"sq{i}") for i in range(2)]
        sqx = sb("sqx", [128, T], BF16)
        B_sqx = Buf("sqx")
        rstd = [sb(f"rstd{i}", [128, T], F32) for i in range(2)]
        B_rstd = [Buf(f"rstd{i}") for i in range(2)]
        cq = sb("cq", [128, 2, T], F32); B_cq = Buf("cq")
        cqn = sb("cqn", [128, 2, T], BF16); B_cqn = Buf("cqn")
        ckv = sb("ckv", [128, T], F32); B_ckv = Buf("ckv")
        ckvn = sb("ckvn", [128, T], BF16); B_ckvn = Buf("ckvn")
        ropeC = sb("ropeC", [128, T], F32)
        ropeS = sb("ropeS", [128, T], F32)
        B_rope = Buf("rope")
        rt1 = [sb(f"rt1_{i}", [128, T], F32) for i in range(2)]
        rt2 = [sb(f"rt2_{i}", [128, T], F32) for i in range(2)]
        B_rt1 = [Buf() for _ in range(2)]
        B_rt2 = [Buf() for _ in range(2)]
        zp = sb("zp", [128, 2, H16 + T], F32); B_zp = Buf("zp")
        sA = sb("sA", [128, 2, H16 + T], F32); B_sA = Buf("sA")
        sB = sb("sB", [128, 2, H16 + T], F32); B_sB = Buf("sB")
        pl = sb("pl", [128, 2, T], BF16); B_pl = Buf("pl")
        ptmp = sb("ptmp", [128, 2, H16], F32); B_ptmp = Buf("ptmp")
        su = sb("su", [128, 2, T], BF16); B_su = Buf("su")
        svsq = sb("svsq", [128, 256], F32); B_svsq = Buf("svsq")
        svss = sb("svss", [128, 2], F32); B_svss = Buf("svss")
        vn = sb("vn", [128, TB, 256], BF16); B_vn = Buf("vn")
        sgt = [sb(f"sgt{i}", [128, 128], F32) for i in range(2)]
        B_sgt = [Buf() for _ in range(2)]
        cvb = sb("cvb", [128, 2, T], BF16); B_cvb = Buf("cvb")
        cvx = sb("cvx", [128, 2, T], F32); B_cvx = Buf("cvx")
        zc = sb("zc", [128, 2, 2 + T], F32); B_zc = Buf("zc")
        yc = sb("yc", [128, 2, T], F32); B_yc = Buf("yc")
        QT = [sb(f"QT{h}", [128, T], BF16) for h in range(NH)]
        B_QT = [Buf(f"QT{h}") for h in range(NH)]
        NPB = 4
        Pb = [sb(f"Pb{i}", [128, T], BF16) for i in range(NPB)]
        B_Pb = [Buf() for _ in range(NPB)]
        rcp = sb("rcp", [128, TB], F32); B_rcp = Buf("rcp")
        Otok = sb("Otok", [128, TB, 512], BF16); B_Otok = Buf("Otok")
        brT = sb("brT", [128, 10, T], BF16)
        B_br = [Buf(f"br{c}") for c in range(10)]
        sg = [sb(f"sg{i}", [128, T], F32) for i in range(4)]
        B_sg = [Buf() for _ in range(4)]
        macc = sb("macc", [128, T], F32); B_macc = Buf("macc")
        mtmp = [sb(f"mtmp{i}", [128, T], F32) for i in range(2)]
        B_mtmp = [Buf() for _ in range(2)]
        mT = sb("mT", [128, 8, T], BF16); B_mT = [Buf(f"mT{d}") for d in range(8)]
        ysb = sb("ysb", [128, 8, T], F32)
        B_ysb = [Buf(f"y{d}") for d in range(8)]
        aT = sb("aT", [128, NFC, T], BF16)
        B_aT = [Buf(f"a{c}") for c in range(NFC)]
        sl = [sb(f"sl{i}", [128, T], F32) for i in range(2)]
        B_sl = [Buf() for _ in range(2)]

        NRING = 5
        psr = [st.enter_context(nc.psum_tensor(f"psr{i}", [128, 512], F32)) for i in range(NRING)]
        B_psr = [Buf(f"psr{i}", excl=True) for i in range(NRING)]
        pso = [st.enter_context(nc.psum_tensor(f"pso{i}", [128, 512], F32)) for i in range(2)]
        B_pso = [Buf(f"pso{i}", excl=True) for i in range(2)]
        psm = st.enter_context(nc.psum_tensor("psm", [128, 512], F32))
        B_psm = Buf("psm", excl=True)
        ring_i = [0]

        def ring():
            i = ring_i[0] % NRING
            ring_i[0] += 1
            return psr[i], B_psr[i]

        cnt = {"rs": 0, "rt": 0, "pb": 0, "sgt": 0, "mt": 0, "sl": 0, "ev": 0}

        def rr(name, n):
            i = cnt[name] % n
            cnt[name] += 1
            return i

        def mm(out, lhsT, rhs, start, stop, reads, writes, **kw):
            P.op("pe", "matmul", out, lhsT=lhsT, rhs=rhs, start=start, stop=stop, **kw,
                 reads=reads, writes=writes)

        def evac_copy(out, in_, reads, writes, eng=None):
            if eng is None:
                eng = "act" if rr("ev", 2) == 0 else "dve"
            if eng == "act":
                P.op("act", "activation", out=out, in_=in_, func=AF.Copy, reads=reads, writes=writes)
            else:
                P.op("dve", "tensor_copy", out, in_, reads=reads, writes=writes)

        def rstd_from_ms(ms_ap, out, Bout, reads):
            P.op("act", "activation", out=out, in_=ms_ap, func=AF.Sqrt, bias=EPS, scale=1.0,
                 reads=reads, writes=[Bout])
            P.op("dve", "reciprocal", out, out, reads=[Bout], writes=[Bout])

        P.op("pool", "memset", identF[:], 0.0, writes=[B_const])
        P.op("pool", "affine_select", out=identF[:], in_=identF[:], pattern=[[-1, 128]],
                                               compare_op=ALU.not_equal, fill=1.0, base=0,
                                               channel_multiplier=1, reads=[B_const], writes=[B_const])
        P.op("pool", "tensor_copy", identB[:], identF[:], reads=[B_const], writes=[B_const])
        P.op("pool", "memset", onesd[:], 1.0 / 1024.0, writes=[B_const])
        P.op("pool", "memset", Vc[:].rearrange("p a b c -> p (a b c)"), 1.0, writes=[B_Vc])
        P.op("sp", "dma_start", out=pcst[:], in_=pcst_d, writes=[B_const], dma=True)
        ones256 = sb("ones256", [128, 128], BF16)
        P.op("pool", "memset", ones256[:], 1.0 / 256.0, writes=[B_const])

        nblk = len(BLOCKS)
        B_wb = [[Buf(f"wb{l}_{b}") for b in range(nblk)] for l in range(L)]

        def cast_blocks(l, b0, b1):
            for b in range(b0, min(b1, nblk)):
                o0, o1 = int(BOFF[b]), int(BOFF[b + 1])
                P.op("pool", "dma_start", out=wb[l, :, o0:o1], in_=wf[l, :, o0:o1],
                     writes=[B_wb[l][b]], dma=True)

        cast_blocks(0, 0, nblk)

        ws_i = [0]

        def load_block(l, b):
            i = ws_i[0] % NWS
            ws_i[0] += 1
            o0, o1 = int(BOFF[b]), int(BOFF[b + 1])
            P.op("sp", "dma_start", out=wslot[i][:, 0:o1 - o0], in_=wb[l, :, o0:o1],
                 reads=[B_wb[l][b]], writes=[B_ws[i]], dma=True)
            return wslot[i], B_ws[i]

        B_xs = [Buf(f"xs{t}") for t in range(NT)]

        vec2 = sb("vec_b", [128, NV], F32)
        vecL = [vec, vec2]
        B_vecL = [Buf("vecA"), Buf("vecB")]
        B_sgu = Buf("sgu_consts")
        G = [(s, l, ti) for s in range(nseq) for l in range(L) for ti in range(ntl)]

        def early_load(s, l, ti):
            cp = (s * L + l) % 2
            vec, B_lay = vecL[cp], B_vecL[cp]
            t0 = ti * T
            xt, Bx = xT[(ti) % 2], B_xT[(ti) % 2]
            P.phase = "load"
            if ti == 0:
                P.op("pool", "dma_start", out=vec[:], in_=vecs[l], writes=[B_lay], dma=True)
            P.phase = "load"
            if l == 0:
                for j in range(TB):
                    r0 = s * S + t0 + j * 128
                    P.op("pool", "dma_start", out=xtok[:], in_=x_in[r0:r0 + 128, :],
                         writes=[B_xtok], dma=True)
                    for hf in range(2):
                        ps, Bp = ring()
                        for kk in range(4):
                            k = hf * 4 + kk
                            P.op("pe", "transpose",
                                ps[:, kk * 128:(kk + 1) * 128], xtok[:, k * 128:(k + 1) * 128], identF[:],
                                reads=[B_xtok, B_const], writes=[Bp])
                        evac_copy(xt[:, hf * 4:(hf + 1) * 4, j * 128:(j + 1) * 128],
                                  ps[:].rearrange("p (a b) -> p a b", a=4), [Bp], Bx[hf * 4:(hf + 1) * 4])
            else:
                P.op("pool", "dma_start",
                    out=xt[:], in_=xs[:, :, t0:t0 + T].rearrange("k p t -> p k t"),
                    reads=[B_xs[ti]], writes=Bx, dma=True)

        def early_rope(s, l, ti):
            t0 = ti * T
            P.op("pool", "dma_start", out=ropeC[64:96, :], in_=ctab[64:96, t0:t0 + T], writes=[B_rope], dma=True)
            P.op("pool", "dma_start", out=ropeS[64:96, :], in_=stab[64:96, t0:t0 + T], writes=[B_rope], dma=True)

        def early_norm(s, l, ti):
            cp = (s * L + l) % 2
            vec, B_lay = vecL[cp], B_vecL[cp]
            xt, Bx = xT[(ti) % 2], B_xT[(ti) % 2]
            P.phase = "A_norm"
            for hf in range(2):
                P.op("act", "activation", out=sq8[:, hf * 4:(hf + 1) * 4, :], in_=xt[:, hf * 4:(hf + 1) * 4, :],
                     func=AF.Square, reads=Bx[hf * 4:(hf + 1) * 4], writes=[B_sq8[hf]])
            for k in range(8):
                mm(pso[1][:, 0:T], onesd[:], sq8[:, k, :], k == 0, k == 7, [B_const, B_sq8[k // 4]], [B_pso[1]])
            r = rr("rt", 2)
            rstd_from_ms(pso[1][:, 0:T], rstd[r][:], B_rstd[r], [B_pso[1]])
            for k in range(8):
                P.op("dve", "scalar_tensor_tensor", out=hT[:, k, :], in0=xt[:, k, :], scalar=vec[:, k:k + 1],
                     in1=rstd[r][:], op0=ALU.mult, op1=ALU.mult, reads=[Bx[k], B_lay, B_rstd[r]], writes=[B_hT[k]])

        def tile_gen(s, l, ti):
            cp = (s * L + l) % 2
            vec, B_lay = vecL[cp], B_vecL[cp]
            if ti == 0:
                P.op("pool", "dma_start", out=sguB[:].rearrange("p a b -> p (a b)"), in_=sgub_d[l], writes=[B_sgu], dma=True)
                P.op("pool", "dma_start", out=gsv[:], in_=gsv_d[l], writes=[B_sgu], dma=True)
                P.op("pool", "dma_start", out=sguW[:].rearrange("p a b -> p (a b)"), in_=sguw_d[l], writes=[B_sgu], dma=True)
                P.op("pool", "memset", sguW[64:128, :, 0:64], 0.0, reads=[B_sgu], writes=[B_sgu])
            t0 = ti * T
            xt, Bx = xT[(ti) % 2], B_xT[(ti) % 2]
            if s == 0 and l + 1 < L:
                per = (nblk + ntl - 1) // ntl
                cast_blocks(l + 1, ti * per, (ti + 1) * per)
            def pre_norm(gcol):
                for hf in range(2):
                    P.op("act", "activation", out=sq8[:, hf * 4:(hf + 1) * 4, :], in_=xt[:, hf * 4:(hf + 1) * 4, :],
                         func=AF.Square, reads=Bx[hf * 4:(hf + 1) * 4], writes=[B_sq8[hf]])
                for k in range(8):
                    mm(psm[:, 0:T], onesd[:], sq8[:, k, :], k == 0, k == 7, [B_const, B_sq8[k // 4]], [B_psm])
                r = rr("rt", 2)
                rstd_from_ms(psm[:, 0:T], rstd[r][:], B_rstd[r], [B_psm])
                for k in range(8):
                    P.op("dve", "scalar_tensor_tensor",
                        out=hT[:, k, :], in0=xt[:, k, :], scalar=vec[:, gcol + k:gcol + k + 1],
                        in1=rstd[r][:], op0=ALU.mult, op1=ALU.mult,
                        reads=[Bx[k], B_lay, B_rstd[r]], writes=[B_hT[k]])

            def post_norm_add(gcol):
                flush_stat()
                r = rr("rt", 2)
                rstd_from_ms(psm[:, 0:T], rstd[r][:], B_rstd[r], [B_psm])
                for d in range(8):
                    m = rr("mt", 2)
                    P.op("dve", "scalar_tensor_tensor",
                        out=mtmp[m][:], in0=ysb[:, d, :], scalar=vec[:, gcol + d:gcol + d + 1],
                        in1=rstd[r][:], op0=ALU.mult, op1=ALU.mult,
                        reads=[B_ysb[d], B_lay, B_rstd[r]], writes=[B_mtmp[m]])
                    P.op("pool" if d % 2 == 0 else "dve", "tensor_tensor",
                        out=xt[:, d, :], in0=xt[:, d, :], in1=mtmp[m][:], op=ALU.add,
                        reads=[Bx[d], B_mtmp[m]], writes=[Bx[d]])

            pend_stat = []

            def flush_stat():
                while pend_stat:
                    i_, d_ = pend_stat.pop(0)
                    mm(psm[:, 0:T], onesd[:], sq[i_][:], d_ == 0, d_ == 7, [B_const, B_sq[i_]], [B_psm])

            def out_chunk(ps, Bp, d):
                flush_stat()
                P.op("dve", "tensor_copy", ysb[:, d, :], ps[:, 0:T], reads=[Bp], writes=[B_ysb[d]])
                i = rr("rs", 2)
                P.op("act", "activation", out=sq[i][:], in_=ps[:, 0:T], func=AF.Square,
                     reads=[Bp], writes=[B_sq[i]])
                pend_stat.append((i, d))

            P.phase = "A_norm_mixin"
            w0, Bw0 = load_block(l, 0)
            w1, Bw1 = load_block(l, 1)
            w2, Bw2 = load_block(l, 2)

            def mixin_mm(ci):
                wt, Bw = (w0, Bw0) if ci < 8 else (w1, Bw1)
                cc = ci if ci < 8 else ci - 8
                M = 96 if ci in (3, 4) else 128
                ps, Bp = ring()
                for k in range(8):
                    o = cc * 1024 + k * 128
                    mm(ps[0:M, 0:T], wt[:, o:o + M], hT[:, k, :], k == 0, k == 7, [Bw, B_hT[k]], [Bp])
                return ps, Bp

            sq3 = [sq[0], sq[1], sqx]
            B_sq3 = [B_sq[0], B_sq[1], B_sqx]
            for c in range(2):
                ps, Bp = mixin_mm(c)
                P.op("dve", "tensor_copy", cq[:, c, :], ps[:, 0:T], reads=[Bp], writes=[B_cq])
                P.op("act", "activation", out=sq3[c][:], in_=ps[:, 0:T], func=AF.Square, reads=[Bp], writes=[B_sq3[c]])
            ps, Bp = mixin_mm(2)
            P.op("dve", "tensor_copy", ckv[:], ps[:, 0:T], reads=[Bp], writes=[B_ckv])
            P.op("act", "activation", out=sq3[2][:], in_=ps[:, 0:T], func=AF.Square, reads=[Bp], writes=[B_sq3[2]])
            psA, BpA = mixin_mm(3)
            a1 = rr("rt", 2)
            P.op("dve", "tensor_tensor", out=rt1[a1][64:96, :], in0=psA[64:96, 0:T], in1=ropeC[64:96, :], op=ALU.mult,
                 reads=[BpA, B_rope], writes=[B_rt1[a1]])
            for c in range(2):
                mm(psm[:, 0:T], ones256[:], sq3[c][:], c == 0, c == 1, [B_const, B_sq3[c]], [B_psm])
            r = rr("rt", 2)
            rstd_from_ms(psm[:, 0:T], rstd[r][:], B_rstd[r], [B_psm])
            for c in range(2):
                P.op("dve", "scalar_tensor_tensor", out=cqn[:, c, :], in0=cq[:, c, :], scalar=vec[:, 32 + c:33 + c],
                     in1=rstd[r][:], op0=ALU.mult, op1=ALU.mult, reads=[B_cq, B_lay, B_rstd[r]], writes=[B_cqn])
            psB, BpB = mixin_mm(4)
            P.op("dve", "tensor_tensor", out=rt2[a1][64:96, :], in0=psB[64:96, 0:T], in1=ropeS[64:96, :], op=ALU.mult,
                 reads=[BpB, B_rope], writes=[B_rt2[a1]])
            mm(psm[:, 0:T], ones256[:], sq3[2][:], True, True, [B_const, B_sq3[2]], [B_psm])
            r = rr("rt", 2)
            P.op("act", "activation", out=rstd[r][:], in_=psm[:, 0:T], func=AF.Sqrt, bias=EPS, scale=2.0,
                 reads=[B_psm], writes=[B_rstd[r]])
            P.op("dve", "reciprocal", rstd[r][:], rstd[r][:], reads=[B_rstd[r]], writes=[B_rstd[r]])
            P.op("dve", "scalar_tensor_tensor", out=ckvn[:], in0=ckv[:], scalar=vec[:, 34:35], in1=rstd[r][:],
                 op0=ALU.mult, op1=ALU.mult, reads=[B_ckv, B_lay, B_rstd[r]], writes=[B_ckvn])
            for h in range(NH):
                P.op("pool", "tensor_tensor",
                    out=KT[h][64:96, t0:t0 + T], in0=rt1[a1][64:96, :], in1=rt2[a1][64:96, :], op=ALU.add,
                    reads=[B_rt1[a1], B_rt2[a1]], writes=[B_KT[h]])
            if ti == 0:
                P.op("pool", "memset", zp[:, :, 0:H16], 0.0, writes=[B_zp])
                P.op("pool", "memset", zc[:, :, 0:2], 0.0, writes=[B_zc])
            else:
                P.op("pool", "tensor_copy", zp[:, :, 0:H16], zp[:, :, T:T + H16], reads=[B_zp], writes=[B_zp])
                P.op("pool", "tensor_copy", zc[:, :, 0:2], zc[:, :, T:T + 2], reads=[B_zc], writes=[B_zc])
            for c in range(2):
                ps, Bp = mixin_mm(5 + c)
                evac_copy(zp[:, c, H16:H16 + T], ps[:, 0:T], [Bp], [B_zp])
            for c in range(2):
                ps, Bp = mixin_mm(7 + c)
                evac_copy(su[:, c, :], ps[:, 0:T], [Bp], [B_su])
            for c in range(2):
                ps, Bp = mixin_mm(9 + c)
                evac_copy(cvb[:, c, :], ps[:, 0:T], [Bp], [B_cvb])
            for c in range(2):
                ps, Bp = mixin_mm(11 + c)
                evac_copy(cvx[:, c, :], ps[:, 0:T], [Bp], [B_cvx])
            for c in range(2):
                ps, Bp = mixin_mm(13 + c)
                P.op("dve", "tensor_tensor", out=zc[:, c, 2:2 + T], in0=ps[:, 0:T],
                                                                 in1=cvx[:, c, :], op=ALU.mult,
                     reads=[Bp, B_cvx], writes=[B_zc])
            P.phase = "sv"
            for j in range(TB):
                ps, Bp = ring()
                for k in range(8):
                    mm(ps[:, 0:256], hT[:, k, j * 128:(j + 1) * 128],
                       w2[:, OFF2_SV + k * 256:OFF2_SV + (k + 1) * 256], k == 0, k == 7, [Bw2, B_hT[k]], [Bp])
                P.op("act", "activation", out=svsq[:], in_=ps[:, 0:256], func=AF.Square,
                     reads=[Bp], writes=[B_svsq])
                P.op("dve", "reduce_sum", out=svss[:, 0:1], in_=svsq[:], axis=AX.X,
                     reads=[B_svsq], writes=[B_svss])
                P.op("act", "activation", out=svss[:, 1:2], in_=svss[:, 0:1], func=AF.Sqrt,
                                                   bias=EPS, scale=1.0 / 256.0,
                     reads=[B_svss], writes=[B_svss])
                P.op("dve", "reciprocal", svss[:, 1:2], svss[:, 1:2], reads=[B_svss], writes=[B_svss])
                P.op("dve", "scalar_tensor_tensor",
                    out=vn[:, j, :], in0=ps[:, 0:256], scalar=svss[:, 1:2], in1=gsv[:],
                    op0=ALU.mult, op1=ALU.mult, reads=[Bp, B_svss, B_sgu], writes=[B_vn])

            P.phase = "mla_proj"
            for h in range(NH):
                p1, Bp1 = ring()
                for k in range(2):
                    o = OFF2_WQ + k * 1536 + h * 192
                    mm(p1[0:96, 0:T], w2[:, o:o + 96], cqn[:, k, :], k == 0, k == 1, [Bw2, B_cqn], [Bp1])
                p2, Bp2 = ring()
                for k in range(2):
                    o = OFF2_WQ + k * 1536 + h * 192 + 96
                    mm(p2[0:96, 0:T], w2[:, o:o + 96], cqn[:, k, :], k == 0, k == 1, [Bw2, B_cqn], [Bp2])
                P.op("act", "activation", out=QT[h][0:64, :], in_=p1[0:64, 0:T], func=AF.Copy,
                     reads=[Bp1], writes=[B_QT[h]])
                a1 = rr("rt", 2)
                P.op("dve", "tensor_tensor", out=rt1[a1][64:96, :], in0=p1[64:96, 0:T],
                                                                   in1=ropeC[64:96, :], op=ALU.mult,
                     reads=[Bp1, B_rope], writes=[B_rt1[a1]])
                P.op("dve", "tensor_tensor", out=rt2[a1][64:96, :], in0=p2[64:96, 0:T],
                                                                   in1=ropeS[64:96, :], op=ALU.mult,
                     reads=[Bp2, B_rope], writes=[B_rt2[a1]])
                P.op("pool", "tensor_tensor",
                    out=QT[h][64:96, :], in0=rt1[a1][64:96, :], in1=rt2[a1][64:96, :], op=ALU.add,
                    reads=[B_rt1[a1], B_rt2[a1]], writes=[B_QT[h]])
            for h in range(NH):
                ps, Bp = ring()
                mm(ps[0:64, 0:T], w2[:, OFF2_WK + h * 64:OFF2_WK + (h + 1) * 64], ckvn[:], True, True,
                   [Bw2, B_ckvn], [Bp])
                evac_copy(KT[h][0:64, t0:t0 + T], ps[0:64, 0:T], [Bp], [B_KT[h]])
            for j in range(TB):
                ps, Bp = ring()
                mm(ps[:, 0:512], ckvn[:, j * 128:(j + 1) * 128], w2[:, OFF2_WV:OFF2_WV + 512], True, True,
                   [Bw2, B_ckvn], [Bp])
                kbi = ti * TB + j
                evac_copy(Vc[:, kbi, :, 0:64], ps[:, 0:512].rearrange("p (h d) -> p h d", h=NH), [Bp], [B_Vc])

            P.phase = "pool"
            HT = H16 + T
            P.op("pool", "tensor_tensor", out=sA[:, :, 1:HT], in0=zp[:, :, 1:HT], in1=zp[:, :, 0:HT - 1], op=ALU.add,
                 reads=[B_zp], writes=[B_sA])
            P.op("pool", "tensor_tensor", out=sB[:, :, 3:HT], in0=sA[:, :, 3:HT], in1=sA[:, :, 1:HT - 2], op=ALU.add,
                 reads=[B_sA], writes=[B_sB])
            P.op("pool", "tensor_tensor", out=sA[:, 1, 7:HT], in0=sB[:, 1, 7:HT], in1=sB[:, 1, 3:HT - 4], op=ALU.add,
                 reads=[B_sB, B_sA], writes=[B_sA])
            P.op("pool", "tensor_tensor", out=sB[64:128, 1, 15:HT], in0=sA[64:128, 1, 15:HT],
                                                   in1=sA[64:128, 1, 7:HT - 8], op=ALU.add,
                 reads=[B_sA, B_sB], writes=[B_sB])
            srcs = [(sA, 0, 0), (sB, 1, 0), (sA, 0, 1), (sB, 1, 1)]
            for g in range(4):
                src, hf, chn = srcs[g]
                rows = slice(hf * 64, (hf + 1) * 64)
                P.op("dve", "scalar_tensor_tensor",
                    out=pl[rows, chn, :], in0=src[rows, chn, H16:HT], scalar=pcst[rows, chn:chn + 1],
                    in1=zp[rows, chn, H16:HT], op0=ALU.mult, op1=ALU.subtract,
                    reads=[B_sA, B_sB, B_zp, B_const], writes=[B_pl])
                if ti == 0:
                    P.op("dve", "tensor_tensor",
                        out=ptmp[rows, chn, :], in0=src[rows, chn, H16:H16 + H16],
                        in1=pcst[rows, 2 + chn * H16:2 + (chn + 1) * H16], op=ALU.mult,
                        reads=[B_sA, B_sB, B_const], writes=[B_ptmp])
                    P.op("dve", "tensor_tensor",
                        out=pl[rows, chn, 0:H16], in0=ptmp[rows, chn, :], in1=zp[rows, chn, H16:H16 + H16],
                        op=ALU.subtract, reads=[B_ptmp, B_zp, B_pl], writes=[B_pl])
            for chn in range(2):
                ps, Bp = ring()
                mm(ps[:, 0:T], w2[:, OFF2_PL + chn * 128:OFF2_PL + (chn + 1) * 128], pl[:, chn, :], True, True,
                   [Bw2, B_pl], [Bp])
                P.op("dve", "tensor_scalar",
                    brT[:, 4 + chn, :], ps[:, 0:T], vec[:, 35 + chn:36 + chn], None, op0=ALU.mult,
                    reads=[Bp, B_lay], writes=[B_br[4 + chn]])

            P.phase = "sgu"
            for j in range(TB):
                for chn in range(2):
                    ps, Bp = ring()
                    for gi in range(2):
                        g = 2 * chn + gi
                        mm(ps[:, gi * 128:(gi + 1) * 128], vn[:, j, chn * 128:(chn + 1) * 128], sguW[:, g, :],
                           True, True, [B_vn, B_sgu], [Bp])
                    for gi in range(2):
                        rows = slice(gi * 64, (gi + 1) * 64)
                        q = rr("sgt", 2)
                        P.op("dve", "tensor_tensor",
                            out=sgt[q][rows, :], in0=ps[rows, gi * 128:(gi + 1) * 128], in1=sguB[rows, chn, :],
                            op=ALU.add, reads=[Bp, B_sgu], writes=[B_sgt[q]])
                        P.op("pool", "tensor_tensor",
                            out=brT[rows, 6 + chn, j * 128:(j + 1) * 128], in0=sgt[q][rows, :],
                            in1=su[rows, chn, j * 128:(j + 1) * 128], op=ALU.mult,
                            reads=[B_sgt[q], B_su], writes=[B_br[6 + chn]])

            P.phase = "conv"
            for chn in range(2):
                P.op("dve", "tensor_scalar",
                    yc[:, chn, :], zc[:, chn, 2:2 + T], vec[:, 37 + chn * 3 + 2:37 + chn * 3 + 3], None, op0=ALU.mult,
                    reads=[B_zc, B_lay], writes=[B_yc])
                for k in (1, 0):
                    P.op("dve", "scalar_tensor_tensor",
                        out=yc[:, chn, :], in0=zc[:, chn, k:k + T], scalar=vec[:, 37 + chn * 3 + k:37 + chn * 3 + k + 1],
                        in1=yc[:, chn, :], op0=ALU.mult, op1=ALU.add,
                        reads=[B_zc, B_lay, B_yc], writes=[B_yc])
                P.op("pool", "tensor_tensor",
                    out=brT[:, 8 + chn, :], in0=yc[:, chn, :], in1=cvb[:, chn, :], op=ALU.mult,
                    reads=[B_yc, B_cvb], writes=[B_br[8 + chn]])

            P.phase = "attn"
            nkb = ti * TB + TB
            steps = [(h, kb) for h in range(NH) for kb in range(nkb)]

            def qk(step):
                h, kb = step
                j0 = max(0, kb - ti * TB)
                c0 = j0 * 128
                ps, Bp = ring()
                mm(ps[:, c0:T], KT[h][0:96, kb * 128:(kb + 1) * 128], QT[h][0:96, c0:T], True, True,
                   [B_KT[h], B_QT[h]], [Bp])
                return ps, Bp

            def softmax_pv(step, ps, Bp):
                h, kb = step
                j0 = max(0, kb - ti * TB)
                c0 = j0 * 128
                pi = rr("pb", NPB)
                pb, Bpb = Pb[pi], B_Pb[pi]
                oi = h % 2
                if kb >= ti * TB:
                    P.op("act", "activation", out=pb[0:64, c0:c0 + 64], in_=ps[0:64, c0:c0 + 64],
                                                       func=AF.Exp, scale=SCALE, reads=[Bp], writes=[Bpb])
                    P.op("act", "activation", out=pb[:, c0 + 64:T], in_=ps[:, c0 + 64:T],
                                                       func=AF.Exp, scale=SCALE, reads=[Bp], writes=[Bpb])
                    P.op("dve", "memset", pb[64:128, c0:c0 + 64], 0.0, writes=[Bpb])
                else:
                    P.op("act", "activation", out=pb[:, c0:T], in_=ps[:, c0:T], func=AF.Exp, scale=SCALE,
                         reads=[Bp], writes=[Bpb])
                for j in range(j0, TB):
                    first = (kb == 0 and j == 0)
                    mm(pso[oi][:, j * 65:(j + 1) * 65], pb[:, j * 128:(j + 1) * 128], Vc[:, kb, h, :],
                       first, kb == ti * TB + j, [Bpb, B_Vc], [B_pso[oi]], skip_group_check=True)
                if kb == nkb - 1:
                    ov = pso[oi][:, 0:TB * 65].rearrange("p (j d) -> p j d", j=TB)
                    P.op("dve", "reciprocal", rcp[:], ov[:, :, 64], reads=[B_pso[oi]], writes=[B_rcp])
                    for j in range(TB):
                        P.op("dve", "tensor_scalar", Otok[:, j, h * 64:(h + 1) * 64], pso[oi][:, j * 65:j * 65 + 64],
                             rcp[:, j:j + 1], None, op0=ALU.mult, reads=[B_pso[oi], B_rcp], writes=[B_Otok])

            LOOK = 3
            pend = []
            nxt_i = 0
            while nxt_i < min(LOOK, len(steps)):
                pend.append((steps[nxt_i], qk(steps[nxt_i])))
                nxt_i += 1
            while pend:
                step, (ps_, Bp_) = pend.pop(0)
                if nxt_i < len(steps):
                    pend.append((steps[nxt_i], qk(steps[nxt_i])))
                    nxt_i += 1
                softmax_pv(step, ps_, Bp_)
            P.phase = "otr"
            for c in range(4):
                ps, Bp = ring()
                psb = ps[:].bitcast(BF16)
                for j in range(TB):
                    P.op("pe", "transpose",
                        psb[:, j * 128:(j + 1) * 128], Otok[:, j, c * 128:(c + 1) * 128], identB[:],
                        reads=[B_Otok, B_const], writes=[Bp])
                evac_copy(brT[:, c, :], psb[:, 0:T], [Bp], [B_br[c]])

            yield
            P.phase = "B"
            for d in range(8):
                wt, Bw = load_block(l, 3 + d)
                for b in range(4):
                    ps, Bp = ring()
                    for k in range(8):
                        o = b * 1024 + k * 128
                        mm(ps[:, 0:T], wt[:, o:o + 128], hT[:, k, :], k == 0, k == 7, [Bw, B_hT[k]], [Bp])
                    P.op("act", "activation", out=sg[b][:], in_=ps[:, 0:T], func=AF.Sigmoid,
                         reads=[Bp], writes=[B_sg[b]])
                chunks = [(0, 4), (4, 6), (6, 8), (8, 10)]
                for b in range(4):
                    ps, Bp = ring()
                    c0, c1 = chunks[b]
                    for c in range(c0, c1):
                        o = 4096 + c * 128
                        mm(ps[:, 0:T], wt[:, o:o + 128], brT[:, c, :], c == c0, c == c1 - 1, [Bw, B_br[c]], [Bp])
                    if b == 0:
                        P.op("dve", "tensor_tensor", out=macc[:], in0=ps[:, 0:T], in1=sg[0][:], op=ALU.mult,
                             reads=[Bp, B_sg[0]], writes=[B_macc])
                    else:
                        m = rr("mt", 2)
                        P.op("dve", "tensor_tensor", out=mtmp[m][:], in0=ps[:, 0:T], in1=sg[b][:], op=ALU.mult,
                             reads=[Bp, B_sg[b]], writes=[B_mtmp[m]])
                        if b < 3:
                            P.op("pool", "tensor_tensor", out=macc[:], in0=macc[:], in1=mtmp[m][:], op=ALU.add,
                                 reads=[B_macc, B_mtmp[m]], writes=[B_macc])
                        else:
                            P.op("pool", "tensor_tensor", out=mT[:, d, :], in0=macc[:], in1=mtmp[m][:], op=ALU.add,
                                 reads=[B_macc, B_mtmp[m]], writes=[B_mT[d]])
            P.phase = "wout"
            wt, Bw = load_block(l, 11)
            for d in range(8):
                ps, Bp = ring()
                for k in range(8):
                    o = k * 1024 + d * 128
                    mm(ps[:, 0:T], wt[:, o:o + 128], mT[:, k, :], k == 0, k == 7, [Bw, B_mT[k]], [Bp])
                out_chunk(ps, Bp, d)
            post_norm_add(8)

            P.phase = "ffn"
            pre_norm(16)
            for gi in range(N_GU):
                wt, Bw = load_block(l, 12 + gi)
                for cc in range(min(FFN_GRP, NFC - gi * FFN_GRP)):
                    c = gi * FFN_GRP + cc
                    psg, Bpg = ring()
                    for k in range(8):
                        o = cc * 2048 + k * 128
                        mm(psg[:, 0:T], wt[:, o:o + 128], hT[:, k, :], k == 0, k == 7, [Bw, B_hT[k]], [Bpg])
                    psu, Bpu = ring()
                    for k in range(8):
                        o = cc * 2048 + 1024 + k * 128
                        mm(psu[:, 0:T], wt[:, o:o + 128], hT[:, k, :], k == 0, k == 7, [Bw, B_hT[k]], [Bpu])
                    q = rr("sl", 2)
                    P.op("act", "activation", out=sl[q][:], in_=psg[:, 0:T], func=AF.Silu,
                         reads=[Bpg], writes=[B_sl[q]])
                    P.op("dve", "tensor_tensor", out=aT[:, c, :], in0=psu[:, 0:T], in1=sl[q][:], op=ALU.mult,
                         reads=[Bpu, B_sl[q]], writes=[B_aT[c]])
            yield
            P.phase = "ffn_dn"
            for dd in range(4):
                wt, Bw = load_block(l, 12 + N_GU + dd)
                for d2 in range(2):
                    d = dd * 2 + d2
                    ps, Bp = ring()
                    for c in range(NFC):
                        o = d2 * 2816 + c * 128
                        mm(ps[:, 0:T], wt[:, o:o + 128], aT[:, c, :], c == 0, c == NFC - 1, [Bw, B_aT[c]], [Bp])
                    out_chunk(ps, Bp, d)
            post_norm_add(24)

            P.phase = "store"
            if l < L - 1:
                P.op("pool", "dma_start",
                    out=xs[:, :, t0:t0 + T].rearrange("k p t -> p k t"), in_=xt[:],
                    reads=Bx, writes=[B_xs[ti]], dma=True)
            else:
                for j in range(TB):
                    for hf in range(2):
                        ps, Bp = ring()
                        for kk in range(4):
                            k = hf * 4 + kk
                            P.op("pe", "transpose",
                                ps[:, kk * 128:(kk + 1) * 128], xt[:, k, j * 128:(j + 1) * 128], identF[:],
                                reads=[Bx[k], B_const], writes=[Bp])
                        evac_copy(xtok[:, hf * 512:(hf + 1) * 512], ps[:], [Bp], [B_xtok])
                    r0 = s * S + t0 + j * 128
                    P.op("pool", "dma_start", out=y_out[r0:r0 + 128, :], in_=xtok[:],
                         reads=[B_xtok], dma=True)

        early_load(*G[0])
        early_rope(*G[0])
        early_norm(*G[0])
        for gi_, g_ in enumerate(G):
            nxt_ = G[gi_ + 1] if gi_ + 1 < len(G) else None
            if nxt_ is not None:
                early_load(*nxt_)
            tg = tile_gen(*g_)
            next(tg)
            if nxt_ is not None:
                early_rope(*nxt_)
            next(tg)
            if nxt_ is not None:
                early_norm(*nxt_)
            for _ in tg:
                pass

        P.emit(st)
    nc._prog_stats = P.stats
    nc._emitted = P.emitted
    return nc


_CACHE = {}


def run(inputs, nseq_total, L):
    nseq = nseq_total // NCORES
    x = np.ascontiguousarray(inputs["x"], dtype=np.float32)
    wf = np.stack([pack_layer(l, inputs) for l in range(L)])
    vecs = np.stack([pack_vecs(l, inputs) for l in range(L)])
    sguw, sgub, gsv = pack_consts(L, inputs)
    ct, stb, pc = const_tables()
    key = (nseq, L)
    if key not in _CACHE:
        _CACHE[key] = build_nc(nseq, L)
    nc = _CACHE[key]
    in_maps = []
    for c in range(NCORES):
        xc = x[c * nseq:(c + 1) * nseq].reshape(nseq * S, D)
        in_maps.append({"x_in": xc, "wf": wf, "vecs": vecs, "sguw": sguw, "sgub": sgub, "gsv": gsv,
                        "ctab": ct, "stab": stb, "pcst": pc})
    res = run_bass_kernel_spmd(nc, in_maps, core_ids=list(range(NCORES)))
    out = np.stack([res.results[c]["y_out"].reshape(nseq, S, D) for c in range(NCORES)])
    return out.reshape(nseq_total, S, D).astype(np.float32)


def kernel(**inputs):
    inputs = {k: np.asarray(v) for k, v in inputs.items()}
    return run(inputs, inputs["x"].shape[0], inputs["w_in"].shape[0])
```

```python
import contextlib
import os
import numpy as np
import concourse.bass as bass
import concourse.mybir as mybir
from concourse.bass_utils import run_bass_kernel_spmd

F32 = mybir.dt.float32
BF16 = mybir.dt.bfloat16
AF = mybir.ActivationFunctionType
ALU = mybir.AluOpType
AX = mybir.AxisListType

D = 1024
S = 2048
NCORES = 8
NH = 8
DFF = 2816
NFC = DFF // 128
EPS = 1e-6
SCALE = 96 ** -0.5
H16 = 16

TB = 2
T = TB * 128
NT = S // T

EPOCH = 24000
N_DMA_SEMS = 8
DMA_EPOCH = 1400


class Buf:
    __slots__ = ("name", "w", "r", "rd", "excl")

    def __init__(self, name="", excl=False):
        self.name = name
        self.excl = excl
        self.w = None
        self.r = {}
        self.rd = []


class Op:
    __slots__ = ("eng", "fn", "waits", "signal", "idx", "is_dma", "dma_ev", "tag")

    def __init__(self, eng, fn, is_dma):
        self.eng = eng
        self.fn = fn
        self.waits = []
        self.signal = False
        self.is_dma = is_dma
        self.dma_ev = None


class Prog:
    ENGS = ("pe", "act", "dve", "pool", "sp")

    def __init__(self, nc):
        self.nc = nc
        self.ops = {e: [] for e in self.ENGS}
        self.dma_count = {}
        self.n_dma = {e: 0 for e in self.ENGS}

    def op(self, eng, method, *args, reads=(), writes=(), dma=False, **kwargs):
        self.total = getattr(self, "total", 0) + 1
        if self.total > int(os.environ.get("KMAX", "1000000000")):
            return None
        o = Op(eng, (method, args, kwargs), dma)
        o.tag = getattr(self, "phase", "")
        if os.environ.get("KLOG"):
            print("OP", self.total, eng, method, kwargs.get("func", ""), [b.name for b in writes])
        lst = self.ops[eng]
        o.idx = len(lst)
        deps = []
        for b in reads:
            if b.w is not None:
                deps.append(b.w)
            if b.excl:
                deps.extend(d for e_, d in b.r.items() if e_ != eng)
        for b in writes:
            if b.w is not None:
                deps.append(b.w)
            deps.extend(b.r.values())
            deps.extend(b.rd)
        seen = set()
        for d in deps:
            if id(d) in seen:
                continue
            seen.add(id(d))
            if (not d.is_dma) and d.eng == eng and eng == "pe":
                continue
            o.waits.append(d)
            d.signal = True
        if dma:
            n = self.n_dma[eng]
            self.n_dma[eng] = n + 1
            key = (eng, n % N_DMA_SEMS, (n // N_DMA_SEMS) // DMA_EPOCH)
            c = self.dma_count.get(key, 0) + 1
            self.dma_count[key] = c
            o.dma_ev = (key, 16 * c)
            o.signal = True
        lst.append(o)
        for b in reads:
            if dma:
                b.rd.append(o)
            else:
                b.r[eng] = o
        for b in writes:
            b.w = o
            b.r = {}
            b.rd = []
        return o

    def emit(self, stack):
        nc = self.nc
        sem_of = {}
        semkeys = []
        for e in self.ENGS:
            k = 0
            for o in self.ops[e]:
                if o.is_dma:
                    sem_of[id(o)] = o.dma_ev
                    if o.dma_ev[0] not in semkeys:
                        semkeys.append(o.dma_ev[0])
                elif o.signal:
                    key = (e, "c", k // EPOCH)
                    sem_of[id(o)] = (key, k % EPOCH + 1)
                    if key not in semkeys:
                        semkeys.append(key)
                    k += 1
        sems = {}
        for key in semkeys:
            sems[key] = stack.enter_context(nc.semaphore("s_" + "_".join(str(x) for x in key)))
        final_dma = {key: 16 * c for key, c in self.dma_count.items()}
        block = stack.enter_context(nc.Block())
        prog = self
        self.stats = {}

        prog.emitted = {e: [] for e in prog.ENGS}

        def run(ename, engobj):
            waited = {}
            nw = 0
            em = prog.emitted[ename]
            for o in prog.ops[ename]:
                for d in o.waits:
                    key, val = sem_of[id(d)]
                    if waited.get(key, 0) >= val:
                        continue
                    waited[key] = val
                    engobj.wait_ge(sems[key], val)
                    em.append(("w", o.tag, d.eng, d.tag))
                    nw += 1
                if o.is_dma:
                    key, val = o.dma_ev
                    if val - 16 > 0 and waited.get(key, 0) < val - 16:
                        waited[key] = val - 16
                        engobj.wait_ge(sems[key], val - 16)
                        nw += 1
                m_, a_, k_ = o.fn
                em.append(("i", o.tag, m_, ""))
                inst = getattr(engobj, m_)(*a_, **k_)
                if o.signal:
                    key, val = sem_of[id(o)]
                    inst.then_inc(sems[key], 16 if o.is_dma else 1)
            if ename == "sp":
                for key, val in final_dma.items():
                    if waited.get(key, 0) < val:
                        engobj.wait_ge(sems[key], val)
            prog.stats[ename] = (len(prog.ops[ename]), nw)

        @block.tensor
        def _(eng):
            run("pe", eng)

        @block.scalar
        def _(eng):
            run("act", eng)

        @block.vector
        def _(eng):
            run("dve", eng)

        @block.gpsimd
        def _(eng):
            run("pool", eng)

        @block.sync
        def _(eng):
            run("sp", eng)


def kmaj(w):
    k, m = w.shape
    return w.reshape(k // 128, 128, m).transpose(1, 0, 2).reshape(128, -1)


O_CQ, O_CKV, O_KR, O_P, O_SU, O_SV, O_CB, O_CC, O_CX, O_G = 0, 256, 384, 416, 672, 928, 1184, 1440, 1696, 1952


def mixin_chunks():
    r = np.arange(32)
    ka = np.concatenate([np.arange(64), O_KR + r, np.arange(32)])
    kb = np.concatenate([np.arange(64), O_KR + (r + 16) % 32, np.arange(32)])
    ch = [np.arange(O_CQ, O_CQ + 128), np.arange(O_CQ + 128, O_CQ + 256), np.arange(O_CKV, O_CKV + 128),
          ka, kb,
          np.arange(O_P, O_P + 128), np.arange(O_P + 128, O_P + 256),
          np.arange(O_SU, O_SU + 128), np.arange(O_SU + 128, O_SU + 256),
          np.arange(O_CB, O_CB + 128), np.arange(O_CB + 128, O_CB + 256),
          np.arange(O_CX, O_CX + 128), np.arange(O_CX + 128, O_CX + 256),
          np.arange(O_CC, O_CC + 128), np.arange(O_CC + 128, O_CC + 256)]
    return ch


N_MIX = 15
BLK_MIX0 = 8 * 1024
BLK_MIX1 = 7 * 1024
OFF2_SV, OFF2_WQ, OFF2_WK, OFF2_WV, OFF2_PL = 0, 2048, 2048 + 3072, 2048 + 3072 + 512, 2048 + 3072 + 1024
BLK_2 = 2048 + 3072 + 512 + 512 + 256
BLK_B = 4096 + 1280
BLK_WO = 8192
FFN_GRP = 4
BLK_GU = FFN_GRP * 2048
N_GU = (NFC + FFN_GRP - 1) // FFN_GRP
BLK_DN = 2 * 2816


def block_sizes():
    b = [BLK_MIX0, BLK_MIX1, BLK_2] + [BLK_B] * 8 + [BLK_WO]
    for i in range(N_GU):
        n = min(FFN_GRP, NFC - i * FFN_GRP)
        b.append(n * 2048)
    b += [BLK_DN] * 4
    return b


BLOCKS = block_sizes()
BOFF = np.concatenate([[0], np.cumsum(BLOCKS)]).astype(np.int64)
NCOLS = int(BOFF[-1])
WSLOT = max(BLOCKS)
NV = 48


def pack_layer(l, inp):
    w_in = inp["w_in"][l]
    segs = []
    ch = mixin_chunks()
    for c in ch:
        segs.append(kmaj(w_in[:, c]))
    segs.append(kmaj(w_in[:, O_SV:O_SV + 256]))
    w_uq = inp["w_uq"][l]
    r = np.arange(32)
    cols = []
    for h in range(NH):
        a = h * 96 + np.arange(96)
        b = np.concatenate([h * 96 + np.arange(64), h * 96 + 64 + (r + 16) % 32])
        cols.append(a)
        cols.append(b)
    cols = np.concatenate(cols)
    segs.append(kmaj(w_uq[:, cols]))
    w_ukv = inp["w_ukv"][l]
    kc = np.concatenate([h * 128 + np.arange(64) for h in range(NH)])
    vc = np.concatenate([h * 128 + 64 + np.arange(64) for h in range(NH)])
    segs.append(w_ukv[:, kc])
    segs.append(w_ukv[:, vc])
    pw = inp["pool_w"][l]
    bd = np.zeros((128, 2, 128), np.float32)
    for g in range(4):
        chn, hf = g // 2, g % 2
        bd[hf * 64:(hf + 1) * 64, chn, hf * 64:(hf + 1) * 64] = pw[g]
    segs.append(bd.reshape(128, 256))
    wbr = np.concatenate([inp["w_br_a"][l], inp["w_br_b"][l], inp["w_br_c"][l], inp["w_br_d"][l]], axis=0)
    for d in range(8):
        for b in range(4):
            segs.append(kmaj(w_in[:, O_G + b * 1024 + d * 128:O_G + b * 1024 + (d + 1) * 128]))
        segs.append(kmaj(wbr[:, d * 128:(d + 1) * 128]))
    segs.append(kmaj(inp["w_out"][l]))
    wg, wu, wd = inp["w_ffn_gate"][l], inp["w_ffn_up"][l], inp["w_ffn_down"][l]
    for c in range(NFC):
        segs.append(kmaj(wg[:, c * 128:(c + 1) * 128]))
        segs.append(kmaj(wu[:, c * 128:(c + 1) * 128]))
    for d in range(8):
        segs.append(kmaj(wd[:, d * 128:(d + 1) * 128]))
    out = np.concatenate(segs, axis=1)
    assert out.shape == (128, NCOLS), (out.shape, NCOLS)
    return np.ascontiguousarray(out, dtype=np.float32)


def pack_vecs(l, inp):
    v = np.zeros((128, NV), np.float32)

    def fm(g, n):
        return g.reshape(n, 128).T
    v[:, 0:8] = fm(inp["g_pre_mix"][l], 8)
    v[:, 8:16] = fm(inp["g_post_mix"][l], 8)
    v[:, 16:24] = fm(inp["g_pre_ffn"][l], 8)
    v[:, 24:32] = fm(inp["g_post_ffn"][l], 8)
    v[:, 32:34] = fm(inp["g_cq"][l], 2)
    v[:, 34:35] = fm(inp["g_ckv"][l], 1)
    v[:, 35:37] = fm(inp["pool_scale"][l], 2)
    cw = inp["conv_w"][l][:, 0, :]
    for chn in range(2):
        for k in range(3):
            v[:, 37 + chn * 3 + k] = cw[k, chn * 128:(chn + 1) * 128]
    return v


def pack_consts(L, inp):
    sguw = np.stack([np.ascontiguousarray(inp["sgu_w"][l].transpose(2, 0, 1)).reshape(128, 512) for l in range(L)])
    sb = np.zeros((L, 128, 2, 128), np.float32)
    for l in range(L):
        for g in range(4):
            sb[l, (g % 2) * 64:(g % 2 + 1) * 64, g // 2, :] = inp["sgu_b"][l][g][None, :]
    gsv = np.stack([np.broadcast_to(inp["g_sgu_v"][l][None, :], (128, 256)) for l in range(L)])
    return sguw.astype(np.float32), sb.reshape(L, 128, 256), np.ascontiguousarray(gsv, dtype=np.float32)


def const_tables():
    inv = 10000.0 ** (-np.arange(0, 32, 2, dtype=np.float32) / 32)
    ang = np.arange(S, dtype=np.float32)[:, None] * inv[None, :]
    cos = np.cos(ang).astype(np.float32).T
    sin = np.sin(ang).astype(np.float32).T
    ct = np.zeros((128, S), np.float32)
    st = np.zeros((128, S), np.float32)
    ct[64:80] = cos
    ct[80:96] = cos
    st[64:80] = -sin
    st[80:96] = sin
    pc = np.zeros((128, 2 + 2 * H16), np.float32)
    wins = (2, 4, 8, 16)
    for g in range(4):
        rows = slice((g % 2) * 64, (g % 2 + 1) * 64)
        chn = g // 2
        pc[rows, chn] = 1.0 / wins[g]
        for t in range(H16):
            pc[rows, 2 + chn * H16 + t] = 1.0 / min(t + 1, wins[g])
    return ct, st, pc


def build_nc(nseq, L, ntl=NT):
    nc = bass.Bass("TRN2", target_bir_lowering=False)
    x_in = nc.dram_tensor("x_in", [nseq * S, D], F32, kind="ExternalInput").ap()
    y_out = nc.dram_tensor("y_out", [nseq * S, D], F32, kind="ExternalOutput").ap()
    wf = nc.dram_tensor("wf", [L, 128, NCOLS], F32, kind="ExternalInput").ap()
    vecs = nc.dram_tensor("vecs", [L, 128, NV], F32, kind="ExternalInput").ap()
    sguw_d = nc.dram_tensor("sguw", [L, 128, 512], F32, kind="ExternalInput").ap()
    sgub_d = nc.dram_tensor("sgub", [L, 128, 256], F32, kind="ExternalInput").ap()
    gsv_d = nc.dram_tensor("gsv", [L, 128, 256], F32, kind="ExternalInput").ap()
    ctab = nc.dram_tensor("ctab", [128, S], F32, kind="ExternalInput").ap()
    stab = nc.dram_tensor("stab", [128, S], F32, kind="ExternalInput").ap()
    pcst_d = nc.dram_tensor("pcst", [128, 2 + 2 * H16], F32, kind="ExternalInput").ap()
    wb = nc.dram_tensor("wb", [L, 128, NCOLS], BF16, kind="Internal").ap()
    xs = nc.dram_tensor("xs", [8, 128, S], F32, kind="Internal").ap()

    P = Prog(nc)
    with contextlib.ExitStack() as st:
        def sb(name, shape, dt):
            return st.enter_context(nc.sbuf_tensor("t_" + name, shape, dt))

        KT = [sb(f"KT{h}", [128, S], BF16) for h in range(NH)]
        B_KT = [Buf(f"KT{h}") for h in range(NH)]
        Vc = sb("Vc", [128, S // 128, NH, 65], BF16)
        B_Vc = Buf("Vc")
        identF = sb("identF", [128, 128], F32)
        identB = sb("identB", [128, 128], BF16)
        onesd = sb("onesd", [128, 128], BF16)
        B_const = Buf("const")
        vec = sb("vec", [128, NV], F32)
        sguW = sb("sguW", [128, 4, 128], BF16)
        sguB = sb("sguB", [128, 2, 128], F32)
        gsv = sb("gsv", [128, 256], F32)
        B_lay = Buf("laycst")
        pcst = sb("pcst", [128, 2 + 2 * H16], F32)
        NWS = 3
        wslot = [sb(f"wslot{i}", [128, WSLOT], BF16) for i in range(NWS)]
        B_ws = [Buf(f"ws{i}") for i in range(NWS)]
        xT = [sb(f"xT{i}", [128, 8, T], F32) for i in range(2)]
        B_xT = [[Buf(f"xT{i}_{k}") for k in range(8)] for i in range(2)]
        xtok = sb("xtok", [128, D], F32)
        B_xtok = Buf("xtok")
        hT = sb("hT", [128, 8, T], BF16)
        B_hT = [Buf(f"hT{k}") for k in range(8)]
        sq8 = sb("sq8", [128, 8, T], BF16)
        B_sq8 = [Buf("sq8a"), Buf("sq8b")]
        sq = [sb(f"sq{i}", [128, T], BF16) for i in range(2)]
        B_sq = [Buf(f"sq{i}") for i in range(2)]
        sqx = sb("sqx", [128, T], BF16)
        B_sqx = Buf("sqx")
        rstd = [sb(f"rstd{i}", [128, T], F32) for i in range(2)]
        B_rstd = [Buf(f"rstd{i}") for i in range(2)]
        cq = sb("cq", [128, 2, T], F32); B_cq = Buf("cq")
        cqn = sb("cqn", [128, 2, T], BF16); B_cqn = Buf("cqn")
        ckv = sb("ckv", [128, T], F32); B_ckv = Buf("ckv")
        ckvn = sb("ckvn", [128, T], BF16); B_ckvn = Buf("ckvn")
        ropeC = sb("ropeC", [128, T], F32)
        ropeS = sb("ropeS", [128, T], F32)
        B_rope = Buf("rope")
        rt1 = [sb(f"rt1_{i}", [128, T], F32) for i in range(2)]
        rt2 = [sb(f"rt2_{i}", [128, T], F32) for i in range(2)]
        B_rt1 = [Buf() for _ in range(2)]
        B_rt2 = [Buf() for _ in range(2)]
        zp = sb("zp", [128, 2, H16 + T], F32); B_zp = Buf("zp")
        sA = sb("sA", [128, 2, H16 + T], F32); B_sA = Buf("sA")
        sB = sb("sB", [128, 2, H16 + T], F32); B_sB = Buf("sB")
        pl = sb("pl", [128, 2, T], BF16); B_pl = Buf("pl")
        ptmp = sb("ptmp", [128, 2, H16], F32); B_ptmp = Buf("ptmp")
        su = sb("su", [128, 2, T], BF16); B_su = Buf("su")
        svsq = sb("svsq", [128, 256], F32); B_svsq = Buf("svsq")
        svss = sb("svss", [128, 2], F32); B_svss = Buf("svss")
        vn = sb("vn", [128, TB, 256], BF16); B_vn = Buf("vn")
        sgt = [sb(f"sgt{i}", [128, 128], F32) for i in range(2)]
        B_sgt = [Buf() for _ in range(2)]
        cvb = sb("cvb", [128, 2, T], BF16); B_cvb = Buf("cvb")
        cvx = sb("cvx", [128, 2, T], F32); B_cvx = Buf("cvx")
        zc = sb("zc", [128, 2, 2 + T], F32); B_zc = Buf("zc")
        yc = sb("yc", [128, 2, T], F32); B_yc = Buf("yc")
        QT = [sb(f"QT{h}", [128, T], BF16) for h in range(NH)]
        B_QT = [Buf(f"QT{h}") for h in range(NH)]
        NPB = 4
        Pb = [sb(f"Pb{i}", [128, T], BF16) for i in range(NPB)]
        B_Pb = [Buf() for _ in range(NPB)]
        rcp = sb("rcp", [128, TB], F32); B_rcp = Buf("rcp")
        Otok = sb("Otok", [128, TB, 512], BF16); B_Otok = Buf("Otok")
        brT = sb("brT", [128, 10, T], BF16)
        B_br = [Buf(f"br{c}") for c in range(10)]
        sg = [sb(f"sg{i}", [128, T], F32) for i in range(4)]
        B_sg = [Buf() for _ in range(4)]
        macc = sb("macc", [128, T], F32); B_macc = Buf("macc")
        mtmp = [sb(f"mtmp{i}", [128, T], F32) for i in range(2)]
        B_mtmp = [Buf() for _ in range(2)]
        mT = sb("mT", [128, 8, T], BF16); B_mT = [Buf(f"mT{d}") for d in range(8)]
        ysb = sb("ysb", [128, 8, T], F32)
        B_ysb = [Buf(f"y{d}") for d in range(8)]
        aT = sb("aT", [128, NFC, T], BF16)
        B_aT = [Buf(f"a{c}") for c in range(NFC)]
        sl = [sb(f"sl{i}", [128, T], F32) for i in range(2)]
        B_sl = [Buf() for _ in range(2)]

        NRING = 5
        psr = [st.enter_context(nc.psum_tensor(f"psr{i}", [128, 512], F32)) for i in range(NRING)]
        B_psr = [Buf(f"psr{i}", excl=True) for i in range(NRING)]
        pso = [st.enter_context(nc.psum_tensor(f"pso{i}", [128, 512], F32)) for i in range(2)]
        B_pso = [Buf(f"pso{i}", excl=True) for i in range(2)]
        psm = st.enter_context(nc.psum_tensor("psm", [128, 512], F32))
        B_psm = Buf("psm", excl=True)
        ring_i = [0]

        def ring():
            i = ring_i[0] % NRING
            ring_i[0] += 1
            return psr[i], B_psr[i]

        cnt = {"rs": 0, "rt": 0, "pb": 0, "sgt": 0, "mt": 0, "sl": 0, "ev": 0}

        def rr(name, n):
            i = cnt[name] % n
            cnt[name] += 1
            return i

        def mm(out, lhsT, rhs, start, stop, reads, writes, **kw):
            P.op("pe", "matmul", out, lhsT=lhsT, rhs=rhs, start=start, stop=stop, **kw,
                 reads=reads, writes=writes)

        def evac_copy(out, in_, reads, writes, eng=None):
            if eng is None:
                eng = "act" if rr("ev", 2) == 0 else "dve"
            if eng == "act":
                P.op("act", "activation", out=out, in_=in_, func=AF.Copy, reads=reads, writes=writes)
            else:
                P.op("dve", "tensor_copy", out, in_, reads=reads, writes=writes)

        def rstd_from_ms(ms_ap, out, Bout, reads, scale=1.0):
            P.op("act", "activation", out=out, in_=ms_ap, func=AF.Ln, bias=EPS, scale=scale,
                 reads=reads, writes=[Bout])
            P.op("act", "activation", out=out, in_=out, func=AF.Exp, scale=-0.5, reads=[Bout], writes=[Bout])

        P.op("pool", "memset", identF[:], 0.0, writes=[B_const])
        P.op("pool", "affine_select", out=identF[:], in_=identF[:], pattern=[[-1, 128]],
                                               compare_op=ALU.not_equal, fill=1.0, base=0,
                                               channel_multiplier=1, reads=[B_const], writes=[B_const])
        P.op("pool", "tensor_copy", identB[:], identF[:], reads=[B_const], writes=[B_const])
        P.op("pool", "memset", onesd[:], 1.0 / 1024.0, writes=[B_const])
        P.op("pool", "memset", Vc[:].rearrange("p a b c -> p (a b c)"), 1.0, writes=[B_Vc])
        P.op("sp", "dma_start", out=pcst[:], in_=pcst_d, writes=[B_const], dma=True)
        ones256 = sb("ones256", [128, 128], BF16)
        P.op("pool", "memset", ones256[:], 1.0 / 256.0, writes=[B_const])

        nblk = len(BLOCKS)
        B_wb = [[Buf(f"wb{l}_{b}") for b in range(nblk)] for l in range(L)]

        def cast_blocks(l, b0, b1):
            for b in range(b0, min(b1, nblk)):
                o0, o1 = int(BOFF[b]), int(BOFF[b + 1])
                P.op("pool", "dma_start", out=wb[l, :, o0:o1], in_=wf[l, :, o0:o1],
                     writes=[B_wb[l][b]], dma=True)

        cast_blocks(0, 0, nblk)

        ws_i = [0]

        def load_block(l, b):
            i = ws_i[0] % NWS
            ws_i[0] += 1
            o0, o1 = int(BOFF[b]), int(BOFF[b + 1])
            P.op("sp", "dma_start", out=wslot[i][:, 0:o1 - o0], in_=wb[l, :, o0:o1],
                 reads=[B_wb[l][b]], writes=[B_ws[i]], dma=True)
            return wslot[i], B_ws[i]

        B_xs = [Buf(f"xs{t}") for t in range(NT)]

        vec2 = sb("vec_b", [128, NV], F32)
        vecL = [vec, vec2]
        B_vecL = [Buf("vecA"), Buf("vecB")]
        B_sgu = Buf("sgu_consts")
        G = [(s, l, ti) for s in range(nseq) for l in range(L) for ti in range(ntl)]

        def early_load(s, l, ti):
            cp = (s * L + l) % 2
            vec, B_lay = vecL[cp], B_vecL[cp]
            t0 = ti * T
            xt, Bx = xT[(ti) % 2], B_xT[(ti) % 2]
            P.phase = "load"
            if ti == 0:
                P.op("pool", "dma_start", out=vec[:], in_=vecs[l], writes=[B_lay], dma=True)
            P.phase = "load"
            if l == 0:
                for j in range(TB):
                    r0 = s * S + t0 + j * 128
                    P.op("pool", "dma_start", out=xtok[:], in_=x_in[r0:r0 + 128, :],
                         writes=[B_xtok], dma=True)
                    for hf in range(2):
                        ps, Bp = ring()
                        for kk in range(4):
                            k = hf * 4 + kk
                            P.op("pe", "transpose",
                                ps[:, kk * 128:(kk + 1) * 128], xtok[:, k * 128:(k + 1) * 128], identF[:],
                                reads=[B_xtok, B_const], writes=[Bp])
                        evac_copy(xt[:, hf * 4:(hf + 1) * 4, j * 128:(j + 1) * 128],
                                  ps[:].rearrange("p (a b) -> p a b", a=4), [Bp], Bx[hf * 4:(hf + 1) * 4])
            else:
                P.op("pool", "dma_start",
                    out=xt[:], in_=xs[:, :, t0:t0 + T].rearrange("k p t -> p k t"),
                    reads=[B_xs[ti]], writes=Bx, dma=True)

        def early_rope(s, l, ti):
            t0 = ti * T
            P.op("pool", "dma_start", out=ropeC[64:96, :], in_=ctab[64:96, t0:t0 + T], writes=[B_rope], dma=True)
            P.op("pool", "dma_start", out=ropeS[64:96, :], in_=stab[64:96, t0:t0 + T], writes=[B_rope], dma=True)

        def early_norm(s, l, ti):
            cp = (s * L + l) % 2
            vec, B_lay = vecL[cp], B_vecL[cp]
            xt, Bx = xT[(ti) % 2], B_xT[(ti) % 2]
            P.phase = "A_norm"
            for hf in range(2):
                P.op("act", "activation", out=sq8[:, hf * 4:(hf + 1) * 4, :], in_=xt[:, hf * 4:(hf + 1) * 4, :],
                     func=AF.Square, reads=Bx[hf * 4:(hf + 1) * 4], writes=[B_sq8[hf]])
            for k in range(8):
                mm(pso[1][:, 0:T], onesd[:], sq8[:, k, :], k == 0, k == 7, [B_const, B_sq8[k // 4]], [B_pso[1]])
            r = rr("rt", 2)
            rstd_from_ms(pso[1][:, 0:T], rstd[r][:], B_rstd[r], [B_pso[1]])
            for k in range(8):
                P.op("dve", "scalar_tensor_tensor", out=hT[:, k, :], in0=xt[:, k, :], scalar=vec[:, k:k + 1],
                     in1=rstd[r][:], op0=ALU.mult, op1=ALU.mult, reads=[Bx[k], B_lay, B_rstd[r]], writes=[B_hT[k]])

        def tile_gen(s, l, ti):
            cp = (s * L + l) % 2
            vec, B_lay = vecL[cp], B_vecL[cp]
            if ti == 0:
                P.op("pool", "dma_start", out=sguB[:].rearrange("p a b -> p (a b)"), in_=sgub_d[l], writes=[B_sgu], dma=True)
                P.op("pool", "dma_start", out=gsv[:], in_=gsv_d[l], writes=[B_sgu], dma=True)
                P.op("pool", "dma_start", out=sguW[:].rearrange("p a b -> p (a b)"), in_=sguw_d[l], writes=[B_sgu], dma=True)
                P.op("pool", "memset", sguW[64:128, :, 0:64], 0.0, reads=[B_sgu], writes=[B_sgu])
            t0 = ti * T
            xt, Bx = xT[(ti) % 2], B_xT[(ti) % 2]
            if s == 0 and l + 1 < L:
                per = (nblk + ntl - 1) // ntl
                cast_blocks(l + 1, ti * per, (ti + 1) * per)
            def pre_norm(gcol):
                for hf in range(2):
                    P.op("act", "activation", out=sq8[:, hf * 4:(hf + 1) * 4, :], in_=xt[:, hf * 4:(hf + 1) * 4, :],
                         func=AF.Square, reads=Bx[hf * 4:(hf + 1) * 4], writes=[B_sq8[hf]])
                for k in range(8):
                    mm(psm[:, 0:T], onesd[:], sq8[:, k, :], k == 0, k == 7, [B_const, B_sq8[k // 4]], [B_psm])
                r = rr("rt", 2)
                rstd_from_ms(psm[:, 0:T], rstd[r][:], B_rstd[r], [B_psm])
                for k in range(8):
                    P.op("dve", "scalar_tensor_tensor",
                        out=hT[:, k, :], in0=xt[:, k, :], scalar=vec[:, gcol + k:gcol + k + 1],
                        in1=rstd[r][:], op0=ALU.mult, op1=ALU.mult,
                        reads=[Bx[k], B_lay, B_rstd[r]], writes=[B_hT[k]])

            def post_norm_add(gcol):
                flush_stat()
                r = rr("rt", 2)
                rstd_from_ms(psm[:, 0:T], rstd[r][:], B_rstd[r], [B_psm])
                for d in range(8):
                    m = rr("mt", 2)
                    P.op("dve", "scalar_tensor_tensor",
                        out=mtmp[m][:], in0=ysb[:, d, :], scalar=vec[:, gcol + d:gcol + d + 1],
                        in1=rstd[r][:], op0=ALU.mult, op1=ALU.mult,
                        reads=[B_ysb[d], B_lay, B_rstd[r]], writes=[B_mtmp[m]])
                    P.op("pool" if d % 2 == 0 else "dve", "tensor_tensor",
                        out=xt[:, d, :], in0=xt[:, d, :], in1=mtmp[m][:], op=ALU.add,
                        reads=[Bx[d], B_mtmp[m]], writes=[Bx[d]])

            pend_stat = []

            def flush_stat():
                while pend_stat:
                    i_, d_ = pend_stat.pop(0)
                    mm(psm[:, 0:T], onesd[:], sq[i_][:], d_ == 0, d_ == 7, [B_const, B_sq[i_]], [B_psm])

            def out_chunk(ps, Bp, d):
                flush_stat()
                P.op("dve", "tensor_copy", ysb[:, d, :], ps[:, 0:T], reads=[Bp], writes=[B_ysb[d]])
                i = rr("rs", 2)
                P.op("act", "activation", out=sq[i][:], in_=ps[:, 0:T], func=AF.Square,
                     reads=[Bp], writes=[B_sq[i]])
                pend_stat.append((i, d))

            P.phase = "A_norm_mixin"
            w0, Bw0 = load_block(l, 0)
            w1, Bw1 = load_block(l, 1)
            w2, Bw2 = load_block(l, 2)

            def mixin_mm(ci):
                wt, Bw = (w0, Bw0) if ci < 8 else (w1, Bw1)
                cc = ci if ci < 8 else ci - 8
                M = 96 if ci in (3, 4) else 128
                ps, Bp = ring()
                for k in range(8):
                    o = cc * 1024 + k * 128
                    mm(ps[0:M, 0:T], wt[:, o:o + M], hT[:, k, :], k == 0, k == 7, [Bw, B_hT[k]], [Bp])
                return ps, Bp

            sq3 = [sq[0], sq[1], sqx]
            B_sq3 = [B_sq[0], B_sq[1], B_sqx]
            for c in range(2):
                ps, Bp = mixin_mm(c)
                P.op("dve", "tensor_copy", cq[:, c, :], ps[:, 0:T], reads=[Bp], writes=[B_cq])
                P.op("act", "activation", out=sq3[c][:], in_=ps[:, 0:T], func=AF.Square, reads=[Bp], writes=[B_sq3[c]])
            ps, Bp = mixin_mm(2)
            P.op("dve", "tensor_copy", ckv[:], ps[:, 0:T], reads=[Bp], writes=[B_ckv])
            P.op("act", "activation", out=sq3[2][:], in_=ps[:, 0:T], func=AF.Square, reads=[Bp], writes=[B_sq3[2]])
            psA, BpA = mixin_mm(3)
            a1 = rr("rt", 2)
            P.op("dve", "tensor_tensor", out=rt1[a1][64:96, :], in0=psA[64:96, 0:T], in1=ropeC[64:96, :], op=ALU.mult,
                 reads=[BpA, B_rope], writes=[B_rt1[a1]])
            for c in range(2):
                mm(psm[:, 0:T], ones256[:], sq3[c][:], c == 0, c == 1, [B_const, B_sq3[c]], [B_psm])
            r = rr("rt", 2)
            rstd_from_ms(psm[:, 0:T], rstd[r][:], B_rstd[r], [B_psm])
            for c in range(2):
                P.op("dve", "scalar_tensor_tensor", out=cqn[:, c, :], in0=cq[:, c, :], scalar=vec[:, 32 + c:33 + c],
                     in1=rstd[r][:], op0=ALU.mult, op1=ALU.mult, reads=[B_cq, B_lay, B_rstd[r]], writes=[B_cqn])
            psB, BpB = mixin_mm(4)
            P.op("dve", "tensor_tensor", out=rt2[a1][64:96, :], in0=psB[64:96, 0:T], in1=ropeS[64:96, :], op=ALU.mult,
                 reads=[BpB, B_rope], writes=[B_rt2[a1]])
            mm(psm[:, 0:T], ones256[:], sq3[2][:], True, True, [B_const, B_sq3[2]], [B_psm])
            r = rr("rt", 2)
            rstd_from_ms(psm[:, 0:T], rstd[r][:], B_rstd[r], [B_psm], scale=2.0)
            P.op("dve", "scalar_tensor_tensor", out=ckvn[:], in0=ckv[:], scalar=vec[:, 34:35], in1=rstd[r][:],
                 op0=ALU.mult, op1=ALU.mult, reads=[B_ckv, B_lay, B_rstd[r]], writes=[B_ckvn])
            for h in range(NH):
                P.op("pool", "tensor_tensor",
                    out=KT[h][64:96, t0:t0 + T], in0=rt1[a1][64:96, :], in1=rt2[a1][64:96, :], op=ALU.add,
                    reads=[B_rt1[a1], B_rt2[a1]], writes=[B_KT[h]])
            if ti == 0:
                P.op("pool", "memset", zp[:, :, 0:H16], 0.0, writes=[B_zp])
                P.op("pool", "memset", zc[:, :, 0:2], 0.0, writes=[B_zc])
            else:
                P.op("pool", "tensor_copy", zp[:, :, 0:H16], zp[:, :, T:T + H16], reads=[B_zp], writes=[B_zp])
                P.op("pool", "tensor_copy", zc[:, :, 0:2], zc[:, :, T:T + 2], reads=[B_zc], writes=[B_zc])
            for c in range(2):
                ps, Bp = mixin_mm(5 + c)
                evac_copy(zp[:, c, H16:H16 + T], ps[:, 0:T], [Bp], [B_zp])
            for c in range(2):
                ps, Bp = mixin_mm(7 + c)
                evac_copy(su[:, c, :], ps[:, 0:T], [Bp], [B_su])
            for c in range(2):
                ps, Bp = mixin_mm(9 + c)
                evac_copy(cvb[:, c, :], ps[:, 0:T], [Bp], [B_cvb])
            for c in range(2):
                ps, Bp = mixin_mm(11 + c)
                evac_copy(cvx[:, c, :], ps[:, 0:T], [Bp], [B_cvx])
            for c in range(2):
                ps, Bp = mixin_mm(13 + c)
                P.op("dve", "tensor_tensor", out=zc[:, c, 2:2 + T], in0=ps[:, 0:T],
                                                                 in1=cvx[:, c, :], op=ALU.mult,
                     reads=[Bp, B_cvx], writes=[B_zc])
            P.phase = "sv"
            for j in range(TB):
                ps, Bp = ring()
                for k in range(8):
                    mm(ps[:, 0:256], hT[:, k, j * 128:(j + 1) * 128],
                       w2[:, OFF2_SV + k * 256:OFF2_SV + (k + 1) * 256], k == 0, k == 7, [Bw2, B_hT[k]], [Bp])
                P.op("act", "activation", out=svsq[:], in_=ps[:, 0:256], func=AF.Square,
                     reads=[Bp], writes=[B_svsq])
                P.op("dve", "reduce_sum", out=svss[:, 0:1], in_=svsq[:], axis=AX.X,
                     reads=[B_svsq], writes=[B_svss])
                P.op("act", "activation", out=svss[:, 1:2], in_=svss[:, 0:1], func=AF.Sqrt,
                                                   bias=EPS, scale=1.0 / 256.0,
                     reads=[B_svss], writes=[B_svss])
                P.op("dve", "reciprocal", svss[:, 1:2], svss[:, 1:2], reads=[B_svss], writes=[B_svss])
                P.op("dve", "scalar_tensor_tensor",
                    out=vn[:, j, :], in0=ps[:, 0:256], scalar=svss[:, 1:2], in1=gsv[:],
                    op0=ALU.mult, op1=ALU.mult, reads=[Bp, B_svss, B_sgu], writes=[B_vn])

            P.phase = "mla_proj"
            for h in range(NH):
                p1, Bp1 = ring()
                for k in range(2):
                    o = OFF2_WQ + k * 1536 + h * 192
                    mm(p1[0:96, 0:T], w2[:, o:o + 96], cqn[:, k, :], k == 0, k == 1, [Bw2, B_cqn], [Bp1])
                p2, Bp2 = ring()
                for k in range(2):
                    o = OFF2_WQ + k * 1536 + h * 192 + 96
                    mm(p2[0:96, 0:T], w2[:, o:o + 96], cqn[:, k, :], k == 0, k == 1, [Bw2, B_cqn], [Bp2])
                P.op("act", "activation", out=QT[h][0:64, :], in_=p1[0:64, 0:T], func=AF.Copy,
                     reads=[Bp1], writes=[B_QT[h]])
                a1 = rr("rt", 2)
                P.op("dve", "tensor_tensor", out=rt1[a1][64:96, :], in0=p1[64:96, 0:T],
                                                                   in1=ropeC[64:96, :], op=ALU.mult,
                     reads=[Bp1, B_rope], writes=[B_rt1[a1]])
                P.op("dve", "tensor_tensor", out=rt2[a1][64:96, :], in0=p2[64:96, 0:T],
                                                                   in1=ropeS[64:96, :], op=ALU.mult,
                     reads=[Bp2, B_rope], writes=[B_rt2[a1]])
                P.op("pool", "tensor_tensor",
                    out=QT[h][64:96, :], in0=rt1[a1][64:96, :], in1=rt2[a1][64:96, :], op=ALU.add,
                    reads=[B_rt1[a1], B_rt2[a1]], writes=[B_QT[h]])
            for h in range(NH):
                ps, Bp = ring()
                mm(ps[0:64, 0:T], w2[:, OFF2_WK + h * 64:OFF2_WK + (h + 1) * 64], ckvn[:], True, True,
                   [Bw2, B_ckvn], [Bp])
                evac_copy(KT[h][0:64, t0:t0 + T], ps[0:64, 0:T], [Bp], [B_KT[h]])
            for j in range(TB):
                ps, Bp = ring()
                mm(ps[:, 0:512], ckvn[:, j * 128:(j + 1) * 128], w2[:, OFF2_WV:OFF2_WV + 512], True, True,
                   [Bw2, B_ckvn], [Bp])
                kbi = ti * TB + j
                evac_copy(Vc[:, kbi, :, 0:64], ps[:, 0:512].rearrange("p (h d) -> p h d", h=NH), [Bp], [B_Vc])

            P.phase = "pool"
            HT = H16 + T
            P.op("pool", "tensor_tensor", out=sA[:, :, 1:HT], in0=zp[:, :, 1:HT], in1=zp[:, :, 0:HT - 1], op=ALU.add,
                 reads=[B_zp], writes=[B_sA])
            P.op("pool", "tensor_tensor", out=sB[:, :, 3:HT], in0=sA[:, :, 3:HT], in1=sA[:, :, 1:HT - 2], op=ALU.add,
                 reads=[B_sA], writes=[B_sB])
            P.op("pool", "tensor_tensor", out=sA[:, 1, 7:HT], in0=sB[:, 1, 7:HT], in1=sB[:, 1, 3:HT - 4], op=ALU.add,
                 reads=[B_sB, B_sA], writes=[B_sA])
            P.op("pool", "tensor_tensor", out=sB[64:128, 1, 15:HT], in0=sA[64:128, 1, 15:HT],
                                                   in1=sA[64:128, 1, 7:HT - 8], op=ALU.add,
                 reads=[B_sA, B_sB], writes=[B_sB])
            srcs = [(sA, 0, 0), (sB, 1, 0), (sA, 0, 1), (sB, 1, 1)]
            for g in range(4):
                src, hf, chn = srcs[g]
                rows = slice(hf * 64, (hf + 1) * 64)
                P.op("dve", "scalar_tensor_tensor",
                    out=pl[rows, chn, :], in0=src[rows, chn, H16:HT], scalar=pcst[rows, chn:chn + 1],
                    in1=zp[rows, chn, H16:HT], op0=ALU.mult, op1=ALU.subtract,
                    reads=[B_sA, B_sB, B_zp, B_const], writes=[B_pl])
                if ti == 0:
                    P.op("dve", "tensor_tensor",
                        out=ptmp[rows, chn, :], in0=src[rows, chn, H16:H16 + H16],
                        in1=pcst[rows, 2 + chn * H16:2 + (chn + 1) * H16], op=ALU.mult,
                        reads=[B_sA, B_sB, B_const], writes=[B_ptmp])
                    P.op("dve", "tensor_tensor",
                        out=pl[rows, chn, 0:H16], in0=ptmp[rows, chn, :], in1=zp[rows, chn, H16:H16 + H16],
                        op=ALU.subtract, reads=[B_ptmp, B_zp, B_pl], writes=[B_pl])
            for chn in range(2):
                ps, Bp = ring()
                mm(ps[:, 0:T], w2[:, OFF2_PL + chn * 128:OFF2_PL + (chn + 1) * 128], pl[:, chn, :], True, True,
                   [Bw2, B_pl], [Bp])
                P.op("dve", "tensor_scalar",
                    brT[:, 4 + chn, :], ps[:, 0:T], vec[:, 35 + chn:36 + chn], None, op0=ALU.mult,
                    reads=[Bp, B_lay], writes=[B_br[4 + chn]])

            P.phase = "sgu"
            for j in range(TB):
                for chn in range(2):
                    ps, Bp = ring()
                    for gi in range(2):
                        g = 2 * chn + gi
                        mm(ps[:, gi * 128:(gi + 1) * 128], vn[:, j, chn * 128:(chn + 1) * 128], sguW[:, g, :],
                           True, True, [B_vn, B_sgu], [Bp])
                    for gi in range(2):
                        rows = slice(gi * 64, (gi + 1) * 64)
                        q = rr("sgt", 2)
                        P.op("dve", "tensor_tensor",
                            out=sgt[q][rows, :], in0=ps[rows, gi * 128:(gi + 1) * 128], in1=sguB[rows, chn, :],
                            op=ALU.add, reads=[Bp, B_sgu], writes=[B_sgt[q]])
                        P.op("pool", "tensor_tensor",
                            out=brT[rows, 6 + chn, j * 128:(j + 1) * 128], in0=sgt[q][rows, :],
                            in1=su[rows, chn, j * 128:(j + 1) * 128], op=ALU.mult,
                            reads=[B_sgt[q], B_su], writes=[B_br[6 + chn]])

            P.phase = "conv"
            for chn in range(2):
                P.op("dve", "tensor_scalar",
                    yc[:, chn, :], zc[:, chn, 2:2 + T], vec[:, 37 + chn * 3 + 2:37 + chn * 3 + 3], None, op0=ALU.mult,
                    reads=[B_zc, B_lay], writes=[B_yc])
                for k in (1, 0):
                    P.op("dve", "scalar_tensor_tensor",
                        out=yc[:, chn, :], in0=zc[:, chn, k:k + T], scalar=vec[:, 37 + chn * 3 + k:37 + chn * 3 + k + 1],
                        in1=yc[:, chn, :], op0=ALU.mult, op1=ALU.add,
                        reads=[B_zc, B_lay, B_yc], writes=[B_yc])
                P.op("pool", "tensor_tensor",
                    out=brT[:, 8 + chn, :], in0=yc[:, chn, :], in1=cvb[:, chn, :], op=ALU.mult,
                    reads=[B_yc, B_cvb], writes=[B_br[8 + chn]])

            P.phase = "attn"
            nkb = ti * TB + TB
            steps = [(h, kb) for h in range(NH) for kb in range(nkb)]

            def qk(step):
                h, kb = step
                j0 = max(0, kb - ti * TB)
                c0 = j0 * 128
                ps, Bp = ring()
                mm(ps[:, c0:T], KT[h][0:96, kb * 128:(kb + 1) * 128], QT[h][0:96, c0:T], True, True,
                   [B_KT[h], B_QT[h]], [Bp])
                return ps, Bp

            def softmax_pv(step, ps, Bp):
                h, kb = step
                j0 = max(0, kb - ti * TB)
                c0 = j0 * 128
                pi = rr("pb", NPB)
                pb, Bpb = Pb[pi], B_Pb[pi]
                oi = h % 2
                if kb >= ti * TB:
                    P.op("act", "activation", out=pb[0:64, c0:c0 + 64], in_=ps[0:64, c0:c0 + 64],
                                                       func=AF.Exp, scale=SCALE, reads=[Bp], writes=[Bpb])
                    P.op("act", "activation", out=pb[:, c0 + 64:T], in_=ps[:, c0 + 64:T],
                                                       func=AF.Exp, scale=SCALE, reads=[Bp], writes=[Bpb])
                    P.op("dve", "memset", pb[64:128, c0:c0 + 64], 0.0, writes=[Bpb])
                else:
                    P.op("act", "activation", out=pb[:, c0:T], in_=ps[:, c0:T], func=AF.Exp, scale=SCALE,
                         reads=[Bp], writes=[Bpb])
                for j in range(j0, TB):
                    first = (kb == 0 and j == 0)
                    mm(pso[oi][:, j * 65:(j + 1) * 65], pb[:, j * 128:(j + 1) * 128], Vc[:, kb, h, :],
                       first, kb == ti * TB + j, [Bpb, B_Vc], [B_pso[oi]], skip_group_check=True)
                if kb == nkb - 1:
                    ov = pso[oi][:, 0:TB * 65].rearrange("p (j d) -> p j d", j=TB)
                    P.op("dve", "reciprocal", rcp[:], ov[:, :, 64], reads=[B_pso[oi]], writes=[B_rcp])
                    for j in range(TB):
                        P.op("dve", "tensor_scalar", Otok[:, j, h * 64:(h + 1) * 64], pso[oi][:, j * 65:j * 65 + 64],
                             rcp[:, j:j + 1], None, op0=ALU.mult, reads=[B_pso[oi], B_rcp], writes=[B_Otok])

            LOOK = 3
            pend = []
            nxt_i = 0
            while nxt_i < min(LOOK, len(steps)):
                pend.append((steps[nxt_i], qk(steps[nxt_i])))
                nxt_i += 1
            while pend:
                step, (ps_, Bp_) = pend.pop(0)
                if nxt_i < len(steps):
                    pend.append((steps[nxt_i], qk(steps[nxt_i])))
                    nxt_i += 1
                softmax_pv(step, ps_, Bp_)
            P.phase = "otr"
            for c in range(4):
                ps, Bp = ring()
                psb = ps[:].bitcast(BF16)
                for j in range(TB):
                    P.op("pe", "transpose",
                        psb[:, j * 128:(j + 1) * 128], Otok[:, j, c * 128:(c + 1) * 128], identB[:],
                        reads=[B_Otok, B_const], writes=[Bp])
                evac_copy(brT[:, c, :], psb[:, 0:T], [Bp], [B_br[c]])

            yield
            P.phase = "B"
            for d in range(8):
                wt, Bw = load_block(l, 3 + d)
                for b in range(4):
                    ps, Bp = ring()
                    for k in range(8):
                        o = b * 1024 + k * 128
                        mm(ps[:, 0:T], wt[:, o:o + 128], hT[:, k, :], k == 0, k == 7, [Bw, B_hT[k]], [Bp])
                    P.op("act", "activation", out=sg[b][:], in_=ps[:, 0:T], func=AF.Sigmoid,
                         reads=[Bp], writes=[B_sg[b]])
                chunks = [(0, 4), (4, 6), (6, 8), (8, 10)]
                for b in range(4):
                    ps, Bp = ring()
                    c0, c1 = chunks[b]
                    for c in range(c0, c1):
                        o = 4096 + c * 128
                        mm(ps[:, 0:T], wt[:, o:o + 128], brT[:, c, :], c == c0, c == c1 - 1, [Bw, B_br[c]], [Bp])
                    if b == 0:
                        P.op("dve", "tensor_tensor", out=macc[:], in0=ps[:, 0:T], in1=sg[0][:], op=ALU.mult,
                             reads=[Bp, B_sg[0]], writes=[B_macc])
                    else:
                        m = rr("mt", 2)
                        P.op("dve", "tensor_tensor", out=mtmp[m][:], in0=ps[:, 0:T], in1=sg[b][:], op=ALU.mult,
                             reads=[Bp, B_sg[b]], writes=[B_mtmp[m]])
                        if b < 3:
                            P.op("pool", "tensor_tensor", out=macc[:], in0=macc[:], in1=mtmp[m][:], op=ALU.add,
                                 reads=[B_macc, B_mtmp[m]], writes=[B_macc])
                        else:
                            P.op("pool", "tensor_tensor", out=mT[:, d, :], in0=macc[:], in1=mtmp[m][:], op=ALU.add,
                                 reads=[B_macc, B_mtmp[m]], writes=[B_mT[d]])
            P.phase = "wout"
            wt, Bw = load_block(l, 11)
            for d in range(8):
                ps, Bp = ring()
                for k in range(8):
                    o = k * 1024 + d * 128
                    mm(ps[:, 0:T], wt[:, o:o + 128], mT[:, k, :], k == 0, k == 7, [Bw, B_mT[k]], [Bp])
                out_chunk(ps, Bp, d)
            post_norm_add(8)

            P.phase = "ffn"
            pre_norm(16)
            for gi in range(N_GU):
                wt, Bw = load_block(l, 12 + gi)
                for cc in range(min(FFN_GRP, NFC - gi * FFN_GRP)):
                    c = gi * FFN_GRP + cc
                    psg, Bpg = ring()
                    for k in range(8):
                        o = cc * 2048 + k * 128
                        mm(psg[:, 0:T], wt[:, o:o + 128], hT[:, k, :], k == 0, k == 7, [Bw, B_hT[k]], [Bpg])
                    psu, Bpu = ring()
                    for k in range(8):
                        o = cc * 2048 + 1024 + k * 128
                        mm(psu[:, 0:T], wt[:, o:o + 128], hT[:, k, :], k == 0, k == 7, [Bw, B_hT[k]], [Bpu])
                    q = rr("sl", 2)
                    P.op("act", "activation", out=sl[q][:], in_=psg[:, 0:T], func=AF.Silu,
                         reads=[Bpg], writes=[B_sl[q]])
                    P.op("dve", "tensor_tensor", out=aT[:, c, :], in0=psu[:, 0:T], in1=sl[q][:], op=ALU.mult,
                         reads=[Bpu, B_sl[q]], writes=[B_aT[c]])
            yield
            P.phase = "ffn_dn"
            for dd in range(4):
                wt, Bw = load_block(l, 12 + N_GU + dd)
                for d2 in range(2):
                    d = dd * 2 + d2
                    ps, Bp = ring()
                    for c in range(NFC):
                        o = d2 * 2816 + c * 128
                        mm(ps[:, 0:T], wt[:, o:o + 128], aT[:, c, :], c == 0, c == NFC - 1, [Bw, B_aT[c]], [Bp])
                    out_chunk(ps, Bp, d)
            post_norm_add(24)

            P.phase = "store"
            if l < L - 1:
                P.op("pool", "dma_start",
                    out=xs[:, :, t0:t0 + T].rearrange("k p t -> p k t"), in_=xt[:],
                    reads=Bx, writes=[B_xs[ti]], dma=True)
            else:
                for j in range(TB):
                    for hf in range(2):
                        ps, Bp = ring()
                        for kk in range(4):
                            k = hf * 4 + kk
                            P.op("pe", "transpose",
                                ps[:, kk * 128:(kk + 1) * 128], xt[:, k, j * 128:(j + 1) * 128], identF[:],
                                reads=[Bx[k], B_const], writes=[Bp])
                        evac_copy(xtok[:, hf * 512:(hf + 1) * 512], ps[:], [Bp], [B_xtok])
                    r0 = s * S + t0 + j * 128
                    P.op("pool", "dma_start", out=y_out[r0:r0 + 128, :], in_=xtok[:],
                         reads=[B_xtok], dma=True)

        early_load(*G[0])
        early_rope(*G[0])
        early_norm(*G[0])
        for gi_, g_ in enumerate(G):
            nxt_ = G[gi_ + 1] if gi_ + 1 < len(G) else None
            if nxt_ is not None:
                early_load(*nxt_)
            tg = tile_gen(*g_)
            next(tg)
            if nxt_ is not None:
                early_rope(*nxt_)
            next(tg)
            if nxt_ is not None:
                early_norm(*nxt_)
            for _ in tg:
                pass

        P.emit(st)
    nc._prog_stats = P.stats
    nc._emitted = P.emitted
    return nc


_CACHE = {}


def run(inputs, nseq_total, L):
    nseq = nseq_total // NCORES
    x = np.ascontiguousarray(inputs["x"], dtype=np.float32)
    wf = np.stack([pack_layer(l, inputs) for l in range(L)])
    vecs = np.stack([pack_vecs(l, inputs) for l in range(L)])
    sguw, sgub, gsv = pack_consts(L, inputs)
    ct, stb, pc = const_tables()
    key = (nseq, L)
    if key not in _CACHE:
        _CACHE[key] = build_nc(nseq, L)
    nc = _CACHE[key]
    in_maps = []
    for c in range(NCORES):
        xc = x[c * nseq:(c + 1) * nseq].reshape(nseq * S, D)
        in_maps.append({"x_in": xc, "wf": wf, "vecs": vecs, "sguw": sguw, "sgub": sgub, "gsv": gsv,
                        "ctab": ct, "stab": stb, "pcst": pc})
    res = run_bass_kernel_spmd(nc, in_maps, core_ids=list(range(NCORES)))
    out = np.stack([res.results[c]["y_out"].reshape(nseq, S, D) for c in range(NCORES)])
    return out.reshape(nseq_total, S, D).astype(np.float32)


def kernel(**inputs):
    inputs = {k: np.asarray(v) for k, v in inputs.items()}
    return run(inputs, inputs["x"].shape[0], inputs["w_in"].shape[0])
```

```python
import contextlib
import os
import numpy as np
import concourse.bass as bass
import concourse.mybir as mybir
from concourse.bass_utils import run_bass_kernel_spmd

F32 = mybir.dt.float32
BF16 = mybir.dt.bfloat16
AF = mybir.ActivationFunctionType
ALU = mybir.AluOpType
AX = mybir.AxisListType

D = 1024
S = 2048
NCORES = 8
NH = 8
DFF = 2816
NFC = DFF // 128
EPS = 1e-6
SCALE = 96 ** -0.5
H16 = 16

TB = 2
T = TB * 128
NT = S // T

EPOCH = 24000
N_DMA_SEMS = 8
DMA_EPOCH = 1400


class Buf:
    __slots__ = ("name", "w", "r", "rd", "excl")

    def __init__(self, name="", excl=False):
        self.name = name
        self.excl = excl
        self.w = None
        self.r = {}
        self.rd = []


class Op:
    __slots__ = ("eng", "fn", "waits", "signal", "idx", "is_dma", "dma_ev", "tag")

    def __init__(self, eng, fn, is_dma):
        self.eng = eng
        self.fn = fn
        self.waits = []
        self.signal = False
        self.is_dma = is_dma
        self.dma_ev = None


class Prog:
    ENGS = ("pe", "act", "dve", "pool", "sp")

    def __init__(self, nc):
        self.nc = nc
        self.ops = {e: [] for e in self.ENGS}
        self.dma_count = {}
        self.n_dma = {e: 0 for e in self.ENGS}

    def op(self, eng, method, *args, reads=(), writes=(), dma=False, **kwargs):
        self.total = getattr(self, "total", 0) + 1
        if self.total > int(os.environ.get("KMAX", "1000000000")):
            return None
        o = Op(eng, (method, args, kwargs), dma)
        o.tag = getattr(self, "phase", "")
        if os.environ.get("KLOG"):
            print("OP", self.total, eng, method, kwargs.get("func", ""), [b.name for b in writes])
        lst = self.ops[eng]
        o.idx = len(lst)
        deps = []
        for b in reads:
            if b.w is not None:
                deps.append(b.w)
            if b.excl:
                deps.extend(d for e_, d in b.r.items() if e_ != eng)
        for b in writes:
            if b.w is not None:
                deps.append(b.w)
            deps.extend(b.r.values())
            deps.extend(b.rd)
        seen = set()
        for d in deps:
            if id(d) in seen:
                continue
            seen.add(id(d))
            if (not d.is_dma) and d.eng == eng and eng == "pe":
                continue
            o.waits.append(d)
            d.signal = True
        if dma:
            n = self.n_dma[eng]
            self.n_dma[eng] = n + 1
            key = (eng, n % N_DMA_SEMS, (n // N_DMA_SEMS) // DMA_EPOCH)
            c = self.dma_count.get(key, 0) + 1
            self.dma_count[key] = c
            o.dma_ev = (key, 16 * c)
            o.signal = True
        lst.append(o)
        for b in reads:
            if dma:
                b.rd.append(o)
            else:
                b.r[eng] = o
        for b in writes:
            b.w = o
            b.r = {}
            b.rd = []
        return o

    def emit(self, stack):
        nc = self.nc
        sem_of = {}
        semkeys = []
        for e in self.ENGS:
            k = 0
            for o in self.ops[e]:
                if o.is_dma:
                    sem_of[id(o)] = o.dma_ev
                    if o.dma_ev[0] not in semkeys:
                        semkeys.append(o.dma_ev[0])
                elif o.signal:
                    key = (e, "c", k // EPOCH)
                    sem_of[id(o)] = (key, k % EPOCH + 1)
                    if key not in semkeys:
                        semkeys.append(key)
                    k += 1
        sems = {}
        for key in semkeys:
            sems[key] = stack.enter_context(nc.semaphore("s_" + "_".join(str(x) for x in key)))
        final_dma = {key: 16 * c for key, c in self.dma_count.items()}
        block = stack.enter_context(nc.Block())
        prog = self
        self.stats = {}

        prog.emitted = {e: [] for e in prog.ENGS}

        def run(ename, engobj):
            waited = {}
            nw = 0
            em = prog.emitted[ename]
            for o in prog.ops[ename]:
                for d in o.waits:
                    key, val = sem_of[id(d)]
                    if waited.get(key, 0) >= val:
                        continue
                    waited[key] = val
                    engobj.wait_ge(sems[key], val)
                    em.append(("w", o.tag, d.eng, d.tag))
                    nw += 1
                if o.is_dma:
                    key, val = o.dma_ev
                    if val - 16 > 0 and waited.get(key, 0) < val - 16:
                        waited[key] = val - 16
                        engobj.wait_ge(sems[key], val - 16)
                        nw += 1
                m_, a_, k_ = o.fn
                em.append(("i", o.tag, m_, ""))
                inst = getattr(engobj, m_)(*a_, **k_)
                if o.signal:
                    key, val = sem_of[id(o)]
                    inst.then_inc(sems[key], 16 if o.is_dma else 1)
            if ename == "sp":
                for key, val in final_dma.items():
                    if waited.get(key, 0) < val:
                        engobj.wait_ge(sems[key], val)
            prog.stats[ename] = (len(prog.ops[ename]), nw)

        @block.tensor
        def _(eng):
            run("pe", eng)

        @block.scalar
        def _(eng):
            run("act", eng)

        @block.vector
        def _(eng):
            run("dve", eng)

        @block.gpsimd
        def _(eng):
            run("pool", eng)

        @block.sync
        def _(eng):
            run("sp", eng)


def kmaj(w):
    k, m = w.shape
    return w.reshape(k // 128, 128, m).transpose(1, 0, 2).reshape(128, -1)


O_CQ, O_CKV, O_KR, O_P, O_SU, O_SV, O_CB, O_CC, O_CX, O_G = 0, 256, 384, 416, 672, 928, 1184, 1440, 1696, 1952


def mixin_chunks():
    r = np.arange(32)
    ka = np.concatenate([np.arange(64), O_KR + r, np.arange(32)])
    kb = np.concatenate([np.arange(64), O_KR + (r + 16) % 32, np.arange(32)])
    ch = [np.arange(O_CQ, O_CQ + 128), np.arange(O_CQ + 128, O_CQ + 256), np.arange(O_CKV, O_CKV + 128),
          ka, kb,
          np.arange(O_P, O_P + 128), np.arange(O_P + 128, O_P + 256),
          np.arange(O_SU, O_SU + 128), np.arange(O_SU + 128, O_SU + 256),
          np.arange(O_CB, O_CB + 128), np.arange(O_CB + 128, O_CB + 256),
          np.arange(O_CX, O_CX + 128), np.arange(O_CX + 128, O_CX + 256),
          np.arange(O_CC, O_CC + 128), np.arange(O_CC + 128, O_CC + 256)]
    return ch


N_MIX = 15
BLK_MIX0 = 8 * 1024
BLK_MIX1 = 7 * 1024
OFF2_SV, OFF2_WQ, OFF2_WK, OFF2_WV, OFF2_PL = 0, 2048, 2048 + 3072, 2048 + 3072 + 512, 2048 + 3072 + 1024
BLK_2 = 2048 + 3072 + 512 + 512 + 256
BLK_B = 4096 + 1280
BLK_WO = 8192
FFN_GRP = 4
BLK_GU = FFN_GRP * 2048
N_GU = (NFC + FFN_GRP - 1) // FFN_GRP
BLK_DN = 2 * 2816


def block_sizes():
    b = [BLK_MIX0, BLK_MIX1, BLK_2] + [BLK_B] * 8 + [BLK_WO]
    for i in range(N_GU):
        n = min(FFN_GRP, NFC - i * FFN_GRP)
        b.append(n * 2048)
    b += [BLK_DN] * 4
    return b


BLOCKS = block_sizes()
BOFF = np.concatenate([[0], np.cumsum(BLOCKS)]).astype(np.int64)
NCOLS = int(BOFF[-1])
WSLOT = max(BLOCKS)
NV = 48


def pack_layer(l, inp):
    w_in = inp["w_in"][l]
    segs = []
    ch = mixin_chunks()
    for c in ch:
        segs.append(kmaj(w_in[:, c]))
    segs.append(kmaj(w_in[:, O_SV:O_SV + 256]))
    w_uq = inp["w_uq"][l]
    r = np.arange(32)
    cols = []
    for h in range(NH):
        a = h * 96 + np.arange(96)
        b = np.concatenate([h * 96 + np.arange(64), h * 96 + 64 + (r + 16) % 32])
        cols.append(a)
        cols.append(b)
    cols = np.concatenate(cols)
    segs.append(kmaj(w_uq[:, cols]))
    w_ukv = inp["w_ukv"][l]
    kc = np.concatenate([h * 128 + np.arange(64) for h in range(NH)])
    vc = np.concatenate([h * 128 + 64 + np.arange(64) for h in range(NH)])
    segs.append(w_ukv[:, kc])
    segs.append(w_ukv[:, vc])
    pw = inp["pool_w"][l]
    bd = np.zeros((128, 2, 128), np.float32)
    for g in range(4):
        chn, hf = g // 2, g % 2
        bd[hf * 64:(hf + 1) * 64, chn, hf * 64:(hf + 1) * 64] = pw[g]
    segs.append(bd.reshape(128, 256))
    wbr = np.concatenate([inp["w_br_a"][l], inp["w_br_b"][l], inp["w_br_c"][l], inp["w_br_d"][l]], axis=0)
    for d in range(8):
        for b in range(4):
            segs.append(kmaj(w_in[:, O_G + b * 1024 + d * 128:O_G + b * 1024 + (d + 1) * 128]))
        segs.append(kmaj(wbr[:, d * 128:(d + 1) * 128]))
    segs.append(kmaj(inp["w_out"][l]))
    wg, wu, wd = inp["w_ffn_gate"][l], inp["w_ffn_up"][l], inp["w_ffn_down"][l]
    for c in range(NFC):
        segs.append(kmaj(wg[:, c * 128:(c + 1) * 128]))
        segs.append(kmaj(wu[:, c * 128:(c + 1) * 128]))
    for d in range(8):
        segs.append(kmaj(wd[:, d * 128:(d + 1) * 128]))
    out = np.concatenate(segs, axis=1)
    assert out.shape == (128, NCOLS), (out.shape, NCOLS)
    return np.ascontiguousarray(out, dtype=np.float32)


def pack_vecs(l, inp):
    v = np.zeros((128, NV), np.float32)

    def fm(g, n):
        return g.reshape(n, 128).T
    v[:, 0:8] = fm(inp["g_pre_mix"][l], 8)
    v[:, 8:16] = fm(inp["g_post_mix"][l], 8)
    v[:, 16:24] = fm(inp["g_pre_ffn"][l], 8)
    v[:, 24:32] = fm(inp["g_post_ffn"][l], 8)
    v[:, 32:34] = fm(inp["g_cq"][l], 2)
    v[:, 34:35] = fm(inp["g_ckv"][l], 1)
    v[:, 35:37] = fm(inp["pool_scale"][l], 2)
    cw = inp["conv_w"][l][:, 0, :]
    for chn in range(2):
        for k in range(3):
            v[:, 37 + chn * 3 + k] = cw[k, chn * 128:(chn + 1) * 128]
    return v


def pack_consts(L, inp):
    sguw = np.stack([np.ascontiguousarray(inp["sgu_w"][l].transpose(2, 0, 1)).reshape(128, 512) for l in range(L)])
    sb = np.zeros((L, 128, 2, 128), np.float32)
    for l in range(L):
        for g in range(4):
            sb[l, (g % 2) * 64:(g % 2 + 1) * 64, g // 2, :] = inp["sgu_b"][l][g][None, :]
    gsv = np.stack([np.broadcast_to(inp["g_sgu_v"][l][None, :], (128, 256)) for l in range(L)])
    return sguw.astype(np.float32), sb.reshape(L, 128, 256), np.ascontiguousarray(gsv, dtype=np.float32)


def const_tables():
    inv = 10000.0 ** (-np.arange(0, 32, 2, dtype=np.float32) / 32)
    ang = np.arange(S, dtype=np.float32)[:, None] * inv[None, :]
    cos = np.cos(ang).astype(np.float32).T
    sin = np.sin(ang).astype(np.float32).T
    ct = np.zeros((128, S), np.float32)
    st = np.zeros((128, S), np.float32)
    ct[64:80] = cos
    ct[80:96] = cos
    st[64:80] = -sin
    st[80:96] = sin
    pc = np.zeros((128, 2 + 2 * H16), np.float32)
    wins = (2, 4, 8, 16)
    for g in range(4):
        rows = slice((g % 2) * 64, (g % 2 + 1) * 64)
        chn = g // 2
        pc[rows, chn] = 1.0 / wins[g]
        for t in range(H16):
            pc[rows, 2 + chn * H16 + t] = 1.0 / min(t + 1, wins[g])
    return ct, st, pc


def build_nc(nseq, L, ntl=NT):
    nc = bass.Bass("TRN2", target_bir_lowering=False)
    x_in = nc.dram_tensor("x_in", [nseq * S, D], F32, kind="ExternalInput").ap()
    y_out = nc.dram_tensor("y_out", [nseq * S, D], F32, kind="ExternalOutput").ap()
    wf = nc.dram_tensor("wf", [L, 128, NCOLS], F32, kind="ExternalInput").ap()
    vecs = nc.dram_tensor("vecs", [L, 128, NV], F32, kind="ExternalInput").ap()
    sguw_d = nc.dram_tensor("sguw", [L, 128, 512], F32, kind="ExternalInput").ap()
    sgub_d = nc.dram_tensor("sgub", [L, 128, 256], F32, kind="ExternalInput").ap()
    gsv_d = nc.dram_tensor("gsv", [L, 128, 256], F32, kind="ExternalInput").ap()
    ctab = nc.dram_tensor("ctab", [128, S], F32, kind="ExternalInput").ap()
    stab = nc.dram_tensor("stab", [128, S], F32, kind="ExternalInput").ap()
    pcst_d = nc.dram_tensor("pcst", [128, 2 + 2 * H16], F32, kind="ExternalInput").ap()
    wb = nc.dram_tensor("wb", [L, 128, NCOLS], BF16, kind="Internal").ap()
    xs = nc.dram_tensor("xs", [8, 128, S], F32, kind="Internal").ap()

    P = Prog(nc)
    with contextlib.ExitStack() as st:
        def sb(name, shape, dt):
            return st.enter_context(nc.sbuf_tensor("t_" + name, shape, dt))

        KT = [sb(f"KT{h}", [128, S], BF16) for h in range(NH)]
        B_KT = [Buf(f"KT{h}") for h in range(NH)]
        Vc = sb("Vc", [128, S // 128, NH, 65], BF16)
        B_Vc = Buf("Vc")
        identF = sb("identF", [128, 128], F32)
        identB = sb("identB", [128, 128], BF16)
        onesd = sb("onesd", [128, 128], BF16)
        B_const = Buf("const")
        vec = sb("vec", [128, NV], F32)
        sguW = sb("sguW", [128, 4, 128], BF16)
        sguB = sb("sguB", [128, 2, 128], F32)
        gsv = sb("gsv", [128, 256], F32)
        B_lay = Buf("laycst")
        pcst = sb("pcst", [128, 2 + 2 * H16], F32)
        NWS = 3
        wslot = [sb(f"wslot{i}", [128, WSLOT], BF16) for i in range(NWS)]
        B_ws = [Buf(f"ws{i}") for i in range(NWS)]
        xT = [sb(f"xT{i}", [128, 8, T], F32) for i in range(2)]
        B_xT = [[Buf(f"xT{i}_{k}") for k in range(8)] for i in range(2)]
        xtok = sb("xtok", [128, D], F32)
        B_xtok = Buf("xtok")
        hT = sb("hT", [128, 8, T], BF16)
        B_hT = [Buf(f"hT{k}") for k in range(8)]
        sq8 = sb("sq8", [128, 8, T], BF16)
        B_sq8 = [Buf("sq8a"), Buf("sq8b")]
        sq = [sb(f"sq{i}", [128, T], BF16) for i in range(2)]
        B_sq = [Buf(f"sq{i}") for i in range(2)]
        sqx = sb("sqx", [128, T], BF16)
        B_sqx = Buf("sqx")
        rstd = [sb(f"rstd{i}", [128, T], F32) for i in range(2)]
        B_rstd = [Buf(f"rstd{i}") for i in range(2)]
        cq = sb("cq", [128, 2, T], F32); B_cq = Buf("cq")
        cqn = sb("cqn", [128, 2, T], BF16); B_cqn = Buf("cqn")
        ckv = sb("ckv", [128, T], F32); B_ckv = Buf("ckv")
        ckvn = sb("ckvn", [128, T], BF16); B_ckvn = Buf("ckvn")
        ropeC = sb("ropeC", [128, T], F32)
        ropeS = sb("ropeS", [128, T], F32)
        B_rope = Buf("rope")
        rt1 = [sb(f"rt1_{i}", [128, T], F32) for i in range(2)]
        rt2 = [sb(f"rt2_{i}", [128, T], F32) for i in range(2)]
        B_rt1 = [Buf() for _ in range(2)]
        B_rt2 = [Buf() for _ in range(2)]
        zp = sb("zp", [128, 2, H16 + T], F32); B_zp = Buf("zp")
        sA = sb("sA", [128, 2, H16 + T], F32); B_sA = Buf("sA")
        sB = sb("sB", [128, 2, H16 + T], F32); B_sB = Buf("sB")
        pl = sb("pl", [128, 2, T], BF16); B_pl = Buf("pl")
        ptmp = sb("ptmp", [128, 2, H16], F32); B_ptmp = Buf("ptmp")
        su = sb("su", [128, 2, T], BF16); B_su = Buf("su")
        svsq = sb("svsq", [128, 256], F32); B_svsq = Buf("svsq")
        svss = sb("svss", [128, 2], F32); B_svss = Buf("svss")
        vn = sb("vn", [128, TB, 256], BF16); B_vn = Buf("vn")
        sgt = [sb(f"sgt{i}", [128, 128], F32) for i in range(2)]
        B_sgt = [Buf() for _ in range(2)]
        cvb = sb("cvb", [128, 2, T], BF16); B_cvb = Buf("cvb")
        cvx = sb("cvx", [128, 2, T], F32); B_cvx = Buf("cvx")
        zc = sb("zc", [128, 2, 2 + T], F32); B_zc = Buf("zc")
        yc = sb("yc", [128, 2, T], F32); B_yc = Buf("yc")
        QT = [sb(f"QT{h}", [128, T], BF16) for h in range(NH)]
        B_QT = [Buf(f"QT{h}") for h in range(NH)]
        NPB = 4
        Pb = [sb(f"Pb{i}", [128, T], BF16) for i in range(NPB)]
        B_Pb = [Buf() for _ in range(NPB)]
        rcp = sb("rcp", [128, TB], F32); B_rcp = Buf("rcp")
        Otok = sb("Otok", [128, TB, 512], BF16); B_Otok = Buf("Otok")
        brT = sb("brT", [128, 10, T], BF16)
        B_br = [Buf(f"br{c}") for c in range(10)]
        sg = [sb(f"sg{i}", [128, T], F32) for i in range(4)]
        B_sg = [Buf() for _ in range(4)]
        macc = sb("macc", [128, T], F32); B_macc = Buf("macc")
        mtmp = [sb(f"mtmp{i}", [128, T], F32) for i in range(2)]
        B_mtmp = [Buf() for _ in range(2)]
        mT = sb("mT", [128, 8, T], BF16); B_mT = [Buf(f"mT{d}") for d in range(8)]
        ysb = sb("ysb", [128, 8, T], F32)
        B_ysb = [Buf(f"y{d}") for d in range(8)]
        aT = sb("aT", [128, NFC, T], BF16)
        B_aT = [Buf(f"a{c}") for c in range(NFC)]
        sl = [sb(f"sl{i}", [128, T], F32) for i in range(2)]
        B_sl = [Buf() for _ in range(2)]

        NRING = 5
        psr = [st.enter_context(nc.psum_tensor(f"psr{i}", [128, 512], F32)) for i in range(NRING)]
        B_psr = [Buf(f"psr{i}", excl=True) for i in range(NRING)]
        pso = [st.enter_context(nc.psum_tensor(f"pso{i}", [128, 512], F32)) for i in range(2)]
        B_pso = [Buf(f"pso{i}", excl=True) for i in range(2)]
        psm = st.enter_context(nc.psum_tensor("psm", [128, 512], F32))
        B_psm = Buf("psm", excl=True)
        ring_i = [0]

        def ring():
            i = ring_i[0] % NRING
            ring_i[0] += 1
            return psr[i], B_psr[i]

        cnt = {"rs": 0, "rt": 0, "pb": 0, "sgt": 0, "mt": 0, "sl": 0, "ev": 0}

        def rr(name, n):
            i = cnt[name] % n
            cnt[name] += 1
            return i

        def mm(out, lhsT, rhs, start, stop, reads, writes, **kw):
            P.op("pe", "matmul", out, lhsT=lhsT, rhs=rhs, start=start, stop=stop, **kw,
                 reads=reads, writes=writes)

        def evac_copy(out, in_, reads, writes, eng=None):
            if eng is None:
                eng = "act" if rr("ev", 2) == 0 else "dve"
            if eng == "act":
                P.op("act", "activation", out=out, in_=in_, func=AF.Copy, reads=reads, writes=writes)
            else:
                P.op("dve", "tensor_copy", out, in_, reads=reads, writes=writes)

        def rstd_from_ms(ms_ap, out, Bout, reads):
            P.op("act", "activation", out=out, in_=ms_ap, func=AF.Sqrt, bias=EPS, scale=1.0,
                 reads=reads, writes=[Bout])
            P.op("dve", "reciprocal", out, out, reads=[Bout], writes=[Bout])

        P.op("pool", "memset", identF[:], 0.0, writes=[B_const])
        P.op("pool", "affine_select", out=identF[:], in_=identF[:], pattern=[[-1, 128]],
                                               compare_op=ALU.not_equal, fill=1.0, base=0,
                                               channel_multiplier=1, reads=[B_const], writes=[B_const])
        P.op("pool", "tensor_copy", identB[:], identF[:], reads=[B_const], writes=[B_const])
        P.op("pool", "memset", onesd[:], 1.0 / 1024.0, writes=[B_const])
        P.op("pool", "memset", Vc[:].rearrange("p a b c -> p (a b c)"), 1.0, writes=[B_Vc])
        P.op("sp", "dma_start", out=pcst[:], in_=pcst_d, writes=[B_const], dma=True)
        ones256 = sb("ones256", [128, 128], BF16)
        P.op("pool", "memset", ones256[:], 1.0 / 256.0, writes=[B_const])

        nblk = len(BLOCKS)
        B_wb = [[Buf(f"wb{l}_{b}") for b in range(nblk)] for l in range(L)]

        def cast_blocks(l, b0, b1):
            for b in range(b0, min(b1, nblk)):
                o0, o1 = int(BOFF[b]), int(BOFF[b + 1])
                P.op("pool", "dma_start", out=wb[l, :, o0:o1], in_=wf[l, :, o0:o1],
                     writes=[B_wb[l][b]], dma=True)

        cast_blocks(0, 0, nblk)

        ws_i = [0]

        def load_block(l, b):
            i = ws_i[0] % NWS
            ws_i[0] += 1
            o0, o1 = int(BOFF[b]), int(BOFF[b + 1])
            P.op("sp", "dma_start", out=wslot[i][:, 0:o1 - o0], in_=wb[l, :, o0:o1],
                 reads=[B_wb[l][b]], writes=[B_ws[i]], dma=True)
            return wslot[i], B_ws[i]

        B_xs = [Buf(f"xs{t}") for t in range(NT)]

        vec2 = sb("vec_b", [128, NV], F32)
        vecL = [vec, vec2]
        B_vecL = [Buf("vecA"), Buf("vecB")]
        B_sgu = Buf("sgu_consts")
        G = [(s, l, ti) for s in range(nseq) for l in range(L) for ti in range(ntl)]

        def early_load(s, l, ti):
            cp = (s * L + l) % 2
            vec, B_lay = vecL[cp], B_vecL[cp]
            t0 = ti * T
            xt, Bx = xT[(ti) % 2], B_xT[(ti) % 2]
            P.phase = "load"
            if ti == 0:
                P.op("pool", "dma_start", out=vec[:], in_=vecs[l], writes=[B_lay], dma=True)
            P.phase = "load"
            if l == 0:
                for j in range(TB):
                    r0 = s * S + t0 + j * 128
                    P.op("pool", "dma_start", out=xtok[:], in_=x_in[r0:r0 + 128, :],
                         writes=[B_xtok], dma=True)
                    for hf in range(2):
                        ps, Bp = ring()
                        for kk in range(4):
                            k = hf * 4 + kk
                            P.op("pe", "transpose",
                                ps[:, kk * 128:(kk + 1) * 128], xtok[:, k * 128:(k + 1) * 128], identF[:],
                                reads=[B_xtok, B_const], writes=[Bp])
                        evac_copy(xt[:, hf * 4:(hf + 1) * 4, j * 128:(j + 1) * 128],
                                  ps[:].rearrange("p (a b) -> p a b", a=4), [Bp], Bx[hf * 4:(hf + 1) * 4])
            else:
                P.op("pool", "dma_start",
                    out=xt[:], in_=xs[:, :, t0:t0 + T].rearrange("k p t -> p k t"),
                    reads=[B_xs[ti]], writes=Bx, dma=True)

        def early_rope(s, l, ti):
            t0 = ti * T
            P.op("pool", "dma_start", out=ropeC[64:96, :], in_=ctab[64:96, t0:t0 + T], writes=[B_rope], dma=True)
            P.op("pool", "dma_start", out=ropeS[64:96, :], in_=stab[64:96, t0:t0 + T], writes=[B_rope], dma=True)

        def early_norm(s, l, ti):
            cp = (s * L + l) % 2
            vec, B_lay = vecL[cp], B_vecL[cp]
            xt, Bx = xT[(ti) % 2], B_xT[(ti) % 2]
            P.phase = "A_norm"
            for hf in range(2):
                P.op("act", "activation", out=sq8[:, hf * 4:(hf + 1) * 4, :], in_=xt[:, hf * 4:(hf + 1) * 4, :],
                     func=AF.Square, reads=Bx[hf * 4:(hf + 1) * 4], writes=[B_sq8[hf]])
            for k in range(8):
                mm(pso[1][:, 0:T], onesd[:], sq8[:, k, :], k == 0, k == 7, [B_const, B_sq8[k // 4]], [B_pso[1]])
            r = rr("rt", 2)
            rstd_from_ms(pso[1][:, 0:T], rstd[r][:], B_rstd[r], [B_pso[1]])
            for k in range(8):
                P.op("dve", "scalar_tensor_tensor", out=hT[:, k, :], in0=xt[:, k, :], scalar=vec[:, k:k + 1],
                     in1=rstd[r][:], op0=ALU.mult, op1=ALU.mult, reads=[Bx[k], B_lay, B_rstd[r]], writes=[B_hT[k]])

        def tile_gen(s, l, ti):
            cp = (s * L + l) % 2
            vec, B_lay = vecL[cp], B_vecL[cp]
            if ti == 0:
                P.op("pool", "dma_start", out=sguB[:].rearrange("p a b -> p (a b)"), in_=sgub_d[l], writes=[B_sgu], dma=True)
                P.op("pool", "dma_start", out=gsv[:], in_=gsv_d[l], writes=[B_sgu], dma=True)
                P.op("pool", "dma_start", out=sguW[:].rearrange("p a b -> p (a b)"), in_=sguw_d[l], writes=[B_sgu], dma=True)
                P.op("pool", "memset", sguW[64:128, :, 0:64], 0.0, reads=[B_sgu], writes=[B_sgu])
            t0 = ti * T
            xt, Bx = xT[(ti) % 2], B_xT[(ti) % 2]
            if s == 0 and l + 1 < L:
                per = (nblk + ntl - 1) // ntl
                cast_blocks(l + 1, ti * per, (ti + 1) * per)
            def pre_norm(gcol):
                for hf in range(2):
                    P.op("act", "activation", out=sq8[:, hf * 4:(hf + 1) * 4, :], in_=xt[:, hf * 4:(hf + 1) * 4, :],
                         func=AF.Square, reads=Bx[hf * 4:(hf + 1) * 4], writes=[B_sq8[hf]])
                for k in range(8):
                    mm(psm[:, 0:T], onesd[:], sq8[:, k, :], k == 0, k == 7, [B_const, B_sq8[k // 4]], [B_psm])
                r = rr("rt", 2)
                rstd_from_ms(psm[:, 0:T], rstd[r][:], B_rstd[r], [B_psm])
                for k in range(8):
                    P.op("dve", "scalar_tensor_tensor",
                        out=hT[:, k, :], in0=xt[:, k, :], scalar=vec[:, gcol + k:gcol + k + 1],
                        in1=rstd[r][:], op0=ALU.mult, op1=ALU.mult,
                        reads=[Bx[k], B_lay, B_rstd[r]], writes=[B_hT[k]])

            def post_norm_add(gcol):
                flush_stat()
                r = rr("rt", 2)
                rstd_from_ms(psm[:, 0:T], rstd[r][:], B_rstd[r], [B_psm])
                for d in range(8):
                    m = rr("mt", 2)
                    P.op("dve", "scalar_tensor_tensor",
                        out=mtmp[m][:], in0=ysb[:, d, :], scalar=vec[:, gcol + d:gcol + d + 1],
                        in1=rstd[r][:], op0=ALU.mult, op1=ALU.mult,
                        reads=[B_ysb[d], B_lay, B_rstd[r]], writes=[B_mtmp[m]])
                    P.op("pool" if d % 2 == 0 else "dve", "tensor_tensor",
                        out=xt[:, d, :], in0=xt[:, d, :], in1=mtmp[m][:], op=ALU.add,
                        reads=[Bx[d], B_mtmp[m]], writes=[Bx[d]])

            pend_stat = []

            def flush_stat():
                while pend_stat:
                    i_, d_ = pend_stat.pop(0)
                    mm(psm[:, 0:T], onesd[:], sq[i_][:], d_ == 0, d_ == 7, [B_const, B_sq[i_]], [B_psm])

            def out_chunk(ps, Bp, d):
                flush_stat()
                P.op("dve", "tensor_copy", ysb[:, d, :], ps[:, 0:T], reads=[Bp], writes=[B_ysb[d]])
                i = rr("rs", 2)
                P.op("act", "activation", out=sq[i][:], in_=ps[:, 0:T], func=AF.Square,
                     reads=[Bp], writes=[B_sq[i]])
                pend_stat.append((i, d))

            P.phase = "A_norm_mixin"
            if ti == 0:
                P.op("pool", "memset", zp[:, :, 0:H16], 0.0, writes=[B_zp])
                P.op("pool", "memset", zc[:, :, 0:2], 0.0, writes=[B_zc])
            else:
                P.op("pool", "tensor_copy", zp[:, :, 0:H16], zp[:, :, T:T + H16], reads=[B_zp], writes=[B_zp])
                P.op("pool", "tensor_copy", zc[:, :, 0:2], zc[:, :, T:T + 2], reads=[B_zc], writes=[B_zc])
            w0, Bw0 = load_block(l, 0)
            w1, Bw1 = load_block(l, 1)
            w2, Bw2 = load_block(l, 2)

            def mixin_mm(ci):
                wt, Bw = (w0, Bw0) if ci < 8 else (w1, Bw1)
                cc = ci if ci < 8 else ci - 8
                M = 96 if ci in (3, 4) else 128
                ps, Bp = ring()
                for k in range(8):
                    o = cc * 1024 + k * 128
                    mm(ps[0:M, 0:T], wt[:, o:o + M], hT[:, k, :], k == 0, k == 7, [Bw, B_hT[k]], [Bp])
                return ps, Bp

            sq3 = [sq[0], sq[1], sqx]
            B_sq3 = [B_sq[0], B_sq[1], B_sqx]
            for c in range(2):
                ps, Bp = mixin_mm(c)
                P.op("dve", "tensor_copy", cq[:, c, :], ps[:, 0:T], reads=[Bp], writes=[B_cq])
                P.op("act", "activation", out=sq3[c][:], in_=ps[:, 0:T], func=AF.Square, reads=[Bp], writes=[B_sq3[c]])
            ps, Bp = mixin_mm(2)
            P.op("dve", "tensor_copy", ckv[:], ps[:, 0:T], reads=[Bp], writes=[B_ckv])
            P.op("act", "activation", out=sq3[2][:], in_=ps[:, 0:T], func=AF.Square, reads=[Bp], writes=[B_sq3[2]])
            psA, BpA = mixin_mm(3)
            a1 = rr("rt", 2)
            P.op("dve", "tensor_tensor", out=rt1[a1][64:96, :], in0=psA[64:96, 0:T], in1=ropeC[64:96, :], op=ALU.mult,
                 reads=[BpA, B_rope], writes=[B_rt1[a1]])
            for c in range(2):
                mm(psm[:, 0:T], ones256[:], sq3[c][:], c == 0, c == 1, [B_const, B_sq3[c]], [B_psm])
            r = rr("rt", 2)
            rstd_from_ms(psm[:, 0:T], rstd[r][:], B_rstd[r], [B_psm])
            for c in range(2):
                P.op("dve", "scalar_tensor_tensor", out=cqn[:, c, :], in0=cq[:, c, :], scalar=vec[:, 32 + c:33 + c],
                     in1=rstd[r][:], op0=ALU.mult, op1=ALU.mult, reads=[B_cq, B_lay, B_rstd[r]], writes=[B_cqn])
            psB, BpB = mixin_mm(4)
            P.op("dve", "tensor_tensor", out=rt2[a1][64:96, :], in0=psB[64:96, 0:T], in1=ropeS[64:96, :], op=ALU.mult,
                 reads=[BpB, B_rope], writes=[B_rt2[a1]])
            mm(psm[:, 0:T], ones256[:], sq3[2][:], True, True, [B_const, B_sq3[2]], [B_psm])
            r = rr("rt", 2)
            P.op("act", "activation", out=rstd[r][:], in_=psm[:, 0:T], func=AF.Sqrt, bias=EPS, scale=2.0,
                 reads=[B_psm], writes=[B_rstd[r]])
            P.op("dve", "reciprocal", rstd[r][:], rstd[r][:], reads=[B_rstd[r]], writes=[B_rstd[r]])
            P.op("dve", "scalar_tensor_tensor", out=ckvn[:], in0=ckv[:], scalar=vec[:, 34:35], in1=rstd[r][:],
                 op0=ALU.mult, op1=ALU.mult, reads=[B_ckv, B_lay, B_rstd[r]], writes=[B_ckvn])
            for h in range(NH):
                P.op("pool" if h % 2 == 0 else "dve", "tensor_tensor",
                    out=KT[h][64:96, t0:t0 + T], in0=rt1[a1][64:96, :], in1=rt2[a1][64:96, :], op=ALU.add,
                    reads=[B_rt1[a1], B_rt2[a1]], writes=[B_KT[h]])
            for c in range(2):
                ps, Bp = mixin_mm(5 + c)
                evac_copy(zp[:, c, H16:H16 + T], ps[:, 0:T], [Bp], [B_zp])
            P.phase = "pool"
            HT = H16 + T
            P.op("pool", "tensor_tensor", out=sA[:, :, 1:HT], in0=zp[:, :, 1:HT], in1=zp[:, :, 0:HT - 1], op=ALU.add,
                 reads=[B_zp], writes=[B_sA])
            P.op("pool", "tensor_tensor", out=sB[:, :, 3:HT], in0=sA[:, :, 3:HT], in1=sA[:, :, 1:HT - 2], op=ALU.add,
                 reads=[B_sA], writes=[B_sB])
            P.op("pool", "tensor_tensor", out=sA[:, 1, 7:HT], in0=sB[:, 1, 7:HT], in1=sB[:, 1, 3:HT - 4], op=ALU.add,
                 reads=[B_sB, B_sA], writes=[B_sA])
            P.op("pool", "tensor_tensor", out=sB[64:128, 1, 15:HT], in0=sA[64:128, 1, 15:HT],
                                                   in1=sA[64:128, 1, 7:HT - 8], op=ALU.add,
                 reads=[B_sA, B_sB], writes=[B_sB])
            srcs = [(sA, 0, 0), (sB, 1, 0), (sA, 0, 1), (sB, 1, 1)]
            for g in range(4):
                src, hf, chn = srcs[g]
                rows = slice(hf * 64, (hf + 1) * 64)
                P.op("dve", "scalar_tensor_tensor",
                    out=pl[rows, chn, :], in0=src[rows, chn, H16:HT], scalar=pcst[rows, chn:chn + 1],
                    in1=zp[rows, chn, H16:HT], op0=ALU.mult, op1=ALU.subtract,
                    reads=[B_sA, B_sB, B_zp, B_const], writes=[B_pl])
                if ti == 0:
                    P.op("dve", "tensor_tensor",
                        out=ptmp[rows, chn, :], in0=src[rows, chn, H16:H16 + H16],
                        in1=pcst[rows, 2 + chn * H16:2 + (chn + 1) * H16], op=ALU.mult,
                        reads=[B_sA, B_sB, B_const], writes=[B_ptmp])
                    P.op("dve", "tensor_tensor",
                        out=pl[rows, chn, 0:H16], in0=ptmp[rows, chn, :], in1=zp[rows, chn, H16:H16 + H16],
                        op=ALU.subtract, reads=[B_ptmp, B_zp, B_pl], writes=[B_pl])
            P.phase = "A_norm_mixin"
            for c in range(2):
                ps, Bp = mixin_mm(7 + c)
                evac_copy(su[:, c, :], ps[:, 0:T], [Bp], [B_su])
            for c in range(2):
                ps, Bp = mixin_mm(9 + c)
                evac_copy(cvb[:, c, :], ps[:, 0:T], [Bp], [B_cvb])
            for c in range(2):
                ps, Bp = mixin_mm(11 + c)
                evac_copy(cvx[:, c, :], ps[:, 0:T], [Bp], [B_cvx])
            for c in range(2):
                ps, Bp = mixin_mm(13 + c)
                P.op("dve", "tensor_tensor", out=zc[:, c, 2:2 + T], in0=ps[:, 0:T],
                                                                 in1=cvx[:, c, :], op=ALU.mult,
                     reads=[Bp, B_cvx], writes=[B_zc])
            P.phase = "sv"
            for j in range(TB):
                ps, Bp = ring()
                for k in range(8):
                    mm(ps[:, 0:256], hT[:, k, j * 128:(j + 1) * 128],
                       w2[:, OFF2_SV + k * 256:OFF2_SV + (k + 1) * 256], k == 0, k == 7, [Bw2, B_hT[k]], [Bp])
                P.op("act", "activation", out=svsq[:], in_=ps[:, 0:256], func=AF.Square,
                     reads=[Bp], writes=[B_svsq])
                P.op("dve", "reduce_sum", out=svss[:, 0:1], in_=svsq[:], axis=AX.X,
                     reads=[B_svsq], writes=[B_svss])
                P.op("act", "activation", out=svss[:, 1:2], in_=svss[:, 0:1], func=AF.Sqrt,
                                                   bias=EPS, scale=1.0 / 256.0,
                     reads=[B_svss], writes=[B_svss])
                P.op("dve", "reciprocal", svss[:, 1:2], svss[:, 1:2], reads=[B_svss], writes=[B_svss])
                P.op("dve", "scalar_tensor_tensor",
                    out=vn[:, j, :], in0=ps[:, 0:256], scalar=svss[:, 1:2], in1=gsv[:],
                    op0=ALU.mult, op1=ALU.mult, reads=[Bp, B_svss, B_sgu], writes=[B_vn])

            P.phase = "mla_proj"
            for h in range(NH):
                p1, Bp1 = ring()
                for k in range(2):
                    o = OFF2_WQ + k * 1536 + h * 192
                    mm(p1[0:96, 0:T], w2[:, o:o + 96], cqn[:, k, :], k == 0, k == 1, [Bw2, B_cqn], [Bp1])
                p2, Bp2 = ring()
                for k in range(2):
                    o = OFF2_WQ + k * 1536 + h * 192 + 96
                    mm(p2[0:96, 0:T], w2[:, o:o + 96], cqn[:, k, :], k == 0, k == 1, [Bw2, B_cqn], [Bp2])
                P.op("act", "activation", out=QT[h][0:64, :], in_=p1[0:64, 0:T], func=AF.Copy,
                     reads=[Bp1], writes=[B_QT[h]])
                a1 = rr("rt", 2)
                P.op("dve", "tensor_tensor", out=rt1[a1][64:96, :], in0=p1[64:96, 0:T],
                                                                   in1=ropeC[64:96, :], op=ALU.mult,
                     reads=[Bp1, B_rope], writes=[B_rt1[a1]])
                P.op("dve", "tensor_tensor", out=rt2[a1][64:96, :], in0=p2[64:96, 0:T],
                                                                   in1=ropeS[64:96, :], op=ALU.mult,
                     reads=[Bp2, B_rope], writes=[B_rt2[a1]])
                P.op("pool", "tensor_tensor",
                    out=QT[h][64:96, :], in0=rt1[a1][64:96, :], in1=rt2[a1][64:96, :], op=ALU.add,
                    reads=[B_rt1[a1], B_rt2[a1]], writes=[B_QT[h]])
            for h in range(NH):
                ps, Bp = ring()
                mm(ps[0:64, 0:T], w2[:, OFF2_WK + h * 64:OFF2_WK + (h + 1) * 64], ckvn[:], True, True,
                   [Bw2, B_ckvn], [Bp])
                evac_copy(KT[h][0:64, t0:t0 + T], ps[0:64, 0:T], [Bp], [B_KT[h]])
            for j in range(TB):
                ps, Bp = ring()
                mm(ps[:, 0:512], ckvn[:, j * 128:(j + 1) * 128], w2[:, OFF2_WV:OFF2_WV + 512], True, True,
                   [Bw2, B_ckvn], [Bp])
                kbi = ti * TB + j
                evac_copy(Vc[:, kbi, :, 0:64], ps[:, 0:512].rearrange("p (h d) -> p h d", h=NH), [Bp], [B_Vc])

            P.phase = "pool"
            for chn in range(2):
                ps, Bp = ring()
                mm(ps[:, 0:T], w2[:, OFF2_PL + chn * 128:OFF2_PL + (chn + 1) * 128], pl[:, chn, :], True, True,
                   [Bw2, B_pl], [Bp])
                P.op("dve", "tensor_scalar",
                    brT[:, 4 + chn, :], ps[:, 0:T], vec[:, 35 + chn:36 + chn], None, op0=ALU.mult,
                    reads=[Bp, B_lay], writes=[B_br[4 + chn]])

            P.phase = "sgu"
            for j in range(TB):
                for chn in range(2):
                    ps, Bp = ring()
                    for gi in range(2):
                        g = 2 * chn + gi
                        mm(ps[:, gi * 128:(gi + 1) * 128], vn[:, j, chn * 128:(chn + 1) * 128], sguW[:, g, :],
                           True, True, [B_vn, B_sgu], [Bp])
                    for gi in range(2):
                        rows = slice(gi * 64, (gi + 1) * 64)
                        q = rr("sgt", 2)
                        P.op("dve", "tensor_tensor",
                            out=sgt[q][rows, :], in0=ps[rows, gi * 128:(gi + 1) * 128], in1=sguB[rows, chn, :],
                            op=ALU.add, reads=[Bp, B_sgu], writes=[B_sgt[q]])
                        P.op("pool", "tensor_tensor",
                            out=brT[rows, 6 + chn, j * 128:(j + 1) * 128], in0=sgt[q][rows, :],
                            in1=su[rows, chn, j * 128:(j + 1) * 128], op=ALU.mult,
                            reads=[B_sgt[q], B_su], writes=[B_br[6 + chn]])

            P.phase = "conv"
            for chn in range(2):
                P.op("dve", "tensor_scalar",
                    yc[:, chn, :], zc[:, chn, 2:2 + T], vec[:, 37 + chn * 3 + 2:37 + chn * 3 + 3], None, op0=ALU.mult,
                    reads=[B_zc, B_lay], writes=[B_yc])
                for k in (1, 0):
                    P.op("dve", "scalar_tensor_tensor",
                        out=yc[:, chn, :], in0=zc[:, chn, k:k + T], scalar=vec[:, 37 + chn * 3 + k:37 + chn * 3 + k + 1],
                        in1=yc[:, chn, :], op0=ALU.mult, op1=ALU.add,
                        reads=[B_zc, B_lay, B_yc], writes=[B_yc])
                P.op("pool", "tensor_tensor",
                    out=brT[:, 8 + chn, :], in0=yc[:, chn, :], in1=cvb[:, chn, :], op=ALU.mult,
                    reads=[B_yc, B_cvb], writes=[B_br[8 + chn]])

            P.phase = "attn"
            nkb = ti * TB + TB
            steps = [(h, kb) for h in range(NH) for kb in range(nkb)]

            def qk(step):
                h, kb = step
                j0 = max(0, kb - ti * TB)
                c0 = j0 * 128
                ps, Bp = ring()
                mm(ps[:, c0:T], KT[h][0:96, kb * 128:(kb + 1) * 128], QT[h][0:96, c0:T], True, True,
                   [B_KT[h], B_QT[h]], [Bp])
                return ps, Bp

            def softmax_pv(step, ps, Bp):
                h, kb = step
                j0 = max(0, kb - ti * TB)
                c0 = j0 * 128
                pi = rr("pb", NPB)
                pb, Bpb = Pb[pi], B_Pb[pi]
                oi = h % 2
                if kb >= ti * TB:
                    P.op("act", "activation", out=pb[0:64, c0:c0 + 64], in_=ps[0:64, c0:c0 + 64],
                                                       func=AF.Exp, scale=SCALE, reads=[Bp], writes=[Bpb])
                    P.op("act", "activation", out=pb[:, c0 + 64:T], in_=ps[:, c0 + 64:T],
                                                       func=AF.Exp, scale=SCALE, reads=[Bp], writes=[Bpb])
                    P.op("dve", "memset", pb[64:128, c0:c0 + 64], 0.0, writes=[Bpb])
                else:
                    P.op("act", "activation", out=pb[:, c0:T], in_=ps[:, c0:T], func=AF.Exp, scale=SCALE,
                         reads=[Bp], writes=[Bpb])
                for j in range(j0, TB):
                    first = (kb == 0 and j == 0)
                    mm(pso[oi][:, j * 65:(j + 1) * 65], pb[:, j * 128:(j + 1) * 128], Vc[:, kb, h, :],
                       first, kb == ti * TB + j, [Bpb, B_Vc], [B_pso[oi]], skip_group_check=True)
                if kb == nkb - 1:
                    ov = pso[oi][:, 0:TB * 65].rearrange("p (j d) -> p j d", j=TB)
                    P.op("dve", "reciprocal", rcp[:], ov[:, :, 64], reads=[B_pso[oi]], writes=[B_rcp])
                    for j in range(TB):
                        P.op("dve", "tensor_scalar", Otok[:, j, h * 64:(h + 1) * 64], pso[oi][:, j * 65:j * 65 + 64],
                             rcp[:, j:j + 1], None, op0=ALU.mult, reads=[B_pso[oi], B_rcp], writes=[B_Otok])

            LOOK = 3
            pend = []
            nxt_i = 0
            while nxt_i < min(LOOK, len(steps)):
                pend.append((steps[nxt_i], qk(steps[nxt_i])))
                nxt_i += 1
            while pend:
                step, (ps_, Bp_) = pend.pop(0)
                if nxt_i < len(steps):
                    pend.append((steps[nxt_i], qk(steps[nxt_i])))
                    nxt_i += 1
                softmax_pv(step, ps_, Bp_)
            P.phase = "otr"
            for c in range(4):
                ps, Bp = ring()
                psb = ps[:].bitcast(BF16)
                for j in range(TB):
                    P.op("pe", "transpose",
                        psb[:, j * 128:(j + 1) * 128], Otok[:, j, c * 128:(c + 1) * 128], identB[:],
                        reads=[B_Otok, B_const], writes=[Bp])
                evac_copy(brT[:, c, :], psb[:, 0:T], [Bp], [B_br[c]])

            yield
            P.phase = "B"
            for d in range(8):
                wt, Bw = load_block(l, 3 + d)
                for b in range(4):
                    ps, Bp = ring()
                    for k in range(8):
                        o = b * 1024 + k * 128
                        mm(ps[:, 0:T], wt[:, o:o + 128], hT[:, k, :], k == 0, k == 7, [Bw, B_hT[k]], [Bp])
                    P.op("act", "activation", out=sg[b][:], in_=ps[:, 0:T], func=AF.Sigmoid,
                         reads=[Bp], writes=[B_sg[b]])
                chunks = [(0, 4), (4, 6), (6, 8), (8, 10)]
                for b in range(4):
                    ps, Bp = ring()
                    c0, c1 = chunks[b]
                    for c in range(c0, c1):
                        o = 4096 + c * 128
                        mm(ps[:, 0:T], wt[:, o:o + 128], brT[:, c, :], c == c0, c == c1 - 1, [Bw, B_br[c]], [Bp])
                    if b == 0:
                        P.op("dve", "tensor_tensor", out=macc[:], in0=ps[:, 0:T], in1=sg[0][:], op=ALU.mult,
                             reads=[Bp, B_sg[0]], writes=[B_macc])
                    else:
                        m = rr("mt", 2)
                        P.op("dve", "tensor_tensor", out=mtmp[m][:], in0=ps[:, 0:T], in1=sg[b][:], op=ALU.mult,
                             reads=[Bp, B_sg[b]], writes=[B_mtmp[m]])
                        if b < 3:
                            P.op("pool", "tensor_tensor", out=macc[:], in0=macc[:], in1=mtmp[m][:], op=ALU.add,
                                 reads=[B_macc, B_mtmp[m]], writes=[B_macc])
                        else:
                            P.op("pool", "tensor_tensor", out=mT[:, d, :], in0=macc[:], in1=mtmp[m][:], op=ALU.add,
                                 reads=[B_macc, B_mtmp[m]], writes=[B_mT[d]])
            P.phase = "wout"
            wt, Bw = load_block(l, 11)
            for d in range(8):
                ps, Bp = ring()
                for k in range(8):
                    o = k * 1024 + d * 128
                    mm(ps[:, 0:T], wt[:, o:o + 128], mT[:, k, :], k == 0, k == 7, [Bw, B_mT[k]], [Bp])
                out_chunk(ps, Bp, d)
            post_norm_add(8)

            P.phase = "ffn"
            pre_norm(16)
            for gi in range(N_GU):
                wt, Bw = load_block(l, 12 + gi)
                for cc in range(min(FFN_GRP, NFC - gi * FFN_GRP)):
                    c = gi * FFN_GRP + cc
                    psg, Bpg = ring()
                    for k in range(8):
                        o = cc * 2048 + k * 128
                        mm(psg[:, 0:T], wt[:, o:o + 128], hT[:, k, :], k == 0, k == 7, [Bw, B_hT[k]], [Bpg])
                    psu, Bpu = ring()
                    for k in range(8):
                        o = cc * 2048 + 1024 + k * 128
                        mm(psu[:, 0:T], wt[:, o:o + 128], hT[:, k, :], k == 0, k == 7, [Bw, B_hT[k]], [Bpu])
                    q = rr("sl", 2)
                    P.op("act", "activation", out=sl[q][:], in_=psg[:, 0:T], func=AF.Silu,
                         reads=[Bpg], writes=[B_sl[q]])
                    P.op("dve", "tensor_tensor", out=aT[:, c, :], in0=psu[:, 0:T], in1=sl[q][:], op=ALU.mult,
                         reads=[Bpu, B_sl[q]], writes=[B_aT[c]])
            yield
            P.phase = "ffn_dn"
            for dd in range(4):
                wt, Bw = load_block(l, 12 + N_GU + dd)
                for d2 in range(2):
                    d = dd * 2 + d2
                    ps, Bp = ring()
                    for c in range(NFC):
                        o = d2 * 2816 + c * 128
                        mm(ps[:, 0:T], wt[:, o:o + 128], aT[:, c, :], c == 0, c == NFC - 1, [Bw, B_aT[c]], [Bp])
                    out_chunk(ps, Bp, d)
            post_norm_add(24)

            P.phase = "store"
            if l < L - 1:
                P.op("pool", "dma_start",
                    out=xs[:, :, t0:t0 + T].rearrange("k p t -> p k t"), in_=xt[:],
                    reads=Bx, writes=[B_xs[ti]], dma=True)
            else:
                for j in range(TB):
                    for hf in range(2):
                        ps, Bp = ring()
                        for kk in range(4):
                            k = hf * 4 + kk
                            P.op("pe", "transpose",
                                ps[:, kk * 128:(kk + 1) * 128], xt[:, k, j * 128:(j + 1) * 128], identF[:],
                                reads=[Bx[k], B_const], writes=[Bp])
                        evac_copy(xtok[:, hf * 512:(hf + 1) * 512], ps[:], [Bp], [B_xtok])
                    r0 = s * S + t0 + j * 128
                    P.op("pool", "dma_start", out=y_out[r0:r0 + 128, :], in_=xtok[:],
                         reads=[B_xtok], dma=True)

        early_load(*G[0])
        early_rope(*G[0])
        early_norm(*G[0])
        for gi_, g_ in enumerate(G):
            nxt_ = G[gi_ + 1] if gi_ + 1 < len(G) else None
            if nxt_ is not None:
                early_load(*nxt_)
            tg = tile_gen(*g_)
            next(tg)
            if nxt_ is not None:
                early_rope(*nxt_)
            next(tg)
            if nxt_ is not None:
                early_norm(*nxt_)
            for _ in tg:
                pass

        P.emit(st)
    nc._prog_stats = P.stats
    nc._emitted = P.emitted
    return nc


_CACHE = {}


def run(inputs, nseq_total, L):
    nseq = nseq_total // NCORES
    x = np.ascontiguousarray(inputs["x"], dtype=np.float32)
    wf = np.stack([pack_layer(l, inputs) for l in range(L)])
    vecs = np.stack([pack_vecs(l, inputs) for l in range(L)])
    sguw, sgub, gsv = pack_consts(L, inputs)
    ct, stb, pc = const_tables()
    key = (nseq, L)
    if key not in _CACHE:
        _CACHE[key] = build_nc(nseq, L)
    nc = _CACHE[key]
    in_maps = []
    for c in range(NCORES):
        xc = x[c * nseq:(c + 1) * nseq].reshape(nseq * S, D)
        in_maps.append({"x_in": xc, "wf": wf, "vecs": vecs, "sguw": sguw, "sgub": sgub, "gsv": gsv,
                        "ctab": ct, "stab": stb, "pcst": pc})
    res = run_bass_kernel_spmd(nc, in_maps, core_ids=list(range(NCORES)))
    out = np.stack([res.results[c]["y_out"].reshape(nseq, S, D) for c in range(NCORES)])
    return out.reshape(nseq_total, S, D).astype(np.float32)


def kernel(**inputs):
    inputs = {k: np.asarray(v) for k, v in inputs.items()}
    return run(inputs, inputs["x"].shape[0], inputs["w_in"].shape[0])
```
